# Optimizing a Trainium2 kernel written in Bass

```python
import jax, jax.numpy as jnp
from jax import lax
import numpy as np

D_MODEL = 2048
BATCH = 16
SEQ = 256
DEPTH = 2
DEC_BATCH = 8
DEC_SEQ = 4096
PAST_LEN = 512

GRID_W = 64
N_EVEN = (DEPTH + 1) // 2
N_ODD = DEPTH // 2
N_MOD = 9
D_FF = 5632
EPS = 1e-6
Q_BLOCK = 128
H_A = 8
HD_A = 128
D_A = H_A * HD_A
WIN_R = 8
WIN_C = 16
QB_C = 16
KB_C = QB_C + WIN_C
D_RNN = 1024
H_B = 8
BD_B = D_RNN // H_B
CONV_W = 4
LRU_C = 8.0
H_C = 8
DK_C = 256
DV_C = 256
D_C = H_C * DV_C
CHUNK = 128
ROPE_BASE = 10000.0
IN_AB = 3 * D_A + 2 * D_RNN
IN_C = 2 * H_C * DK_C + 2 * D_C + 4 * H_C

kernel_name = 'hybrid_na_rglru_mlstm_diffusion_step'


def rms_norm(x, g):
    xf = x.astype(jnp.float32)
    y = xf * lax.rsqrt(jnp.mean(xf * xf, axis=-1, keepdims=True) + EPS)
    return (y * g.astype(jnp.float32)).astype(x.dtype)


def modulation(cond, w, b):
    m = jax.nn.silu(cond) @ w + b
    return m.reshape(cond.shape[0], N_MOD, -1)


def adaln(x, g, mod, j):
    return rms_norm(x, g) * (1.0 + mod[:, 3 * j + 1][:, None]) + mod[:, 3 * j][:, None]


def macaron_ffn(x, mod, j, g, w_gate, w_up, w_down):
    h = adaln(x, g, mod, j)
    y = (jax.nn.silu(h @ w_gate) * (h @ w_up)) @ w_down
    return x + 0.5 * mod[:, 3 * j + 2][:, None] * y


def ab_project(h, w_in, qg, kg):
    b, t, _ = h.shape
    z = h @ w_in
    qa, ka, va, xb, gb = jnp.split(z, [D_A, 2 * D_A, 3 * D_A, 3 * D_A + D_RNN], axis=-1)
    qa = rms_norm(qa.reshape(b, t, H_A, HD_A), qg)
    ka = rms_norm(ka.reshape(b, t, H_A, HD_A), kg)
    va = va.reshape(b, t, H_A, HD_A)
    return qa, ka, va, xb, gb


def ctx_attention(q, k, v):
    b, l, h, hd = q.shape
    qb = jnp.moveaxis(q.reshape(b, l // Q_BLOCK, Q_BLOCK, h, hd), 1, 0)

    def blk(qq):
        s = jnp.einsum('bqhd,bkhd->bhqk', qq, k).astype(jnp.float32) * (hd ** -0.5)
        p = jax.nn.softmax(s, axis=-1).astype(v.dtype)
        return jnp.einsum('bhqk,bkhd->bqhd', p, v)

    o = lax.map(blk, qb)
    return jnp.moveaxis(o, 0, 1).reshape(b, l, h * hd)


def neighbourhood_attention(q, k, v, ck, cv, rpb):
    b, t, h, hd = q.shape
    rows = t // GRID_W
    kr = min(WIN_R, rows)
    ncb = GRID_W // QB_C
    scale = hd ** -0.5
    starts = np.clip(np.arange(ncb) * QB_C - WIN_C // 2, 0, GRID_W - KB_C)
    qcol = np.arange(GRID_W).reshape(ncb, QB_C)
    c0 = np.clip(qcol - WIN_C // 2, 0, GRID_W - WIN_C)
    kcol = starts[:, None] + np.arange(KB_C)[None, :]
    col_ok = (kcol[:, None, :] >= c0[:, :, None]) & (kcol[:, None, :] < c0[:, :, None] + WIN_C)
    dc_idx = np.clip(kcol[:, None, :] - qcol[:, :, None] + WIN_C - 1, 0, 2 * WIN_C - 2)
    mask = col_ok[None, None, :, :, None, :]
    kg = k.reshape(b, rows, GRID_W, h, hd)
    vg = v.reshape(b, rows, GRID_W, h, hd)
    qg = jnp.moveaxis(q.reshape(b, rows, ncb, QB_C, h, hd), 1, 0)

    def one_row(args):
        r, qr = args
        r0 = jnp.clip(r - WIN_R // 2, 0, rows - kr)
        kw = lax.dynamic_slice_in_dim(kg, r0, kr, axis=1)[:, :, kcol]
        vw = lax.dynamic_slice_in_dim(vg, r0, kr, axis=1)[:, :, kcol]
        s_win = jnp.einsum('bjqhd,brjkhd->bhjqrk', qr, kw).astype(jnp.float32) * scale
        dr_idx = r0 + jnp.arange(kr) - r + (WIN_R - 1)
        bias = jnp.transpose(rpb[:, dr_idx][:, :, dc_idx], (0, 2, 3, 1, 4))
        s_win = jnp.where(mask, s_win + bias[None].astype(jnp.float32), -jnp.inf)
        s_win = s_win.reshape(b, h, ncb, QB_C, kr * KB_C)
        s_ctx = jnp.einsum('bjqhd,blhd->bhjql', qr, ck).astype(jnp.float32) * scale
        p = jax.nn.softmax(jnp.concatenate([s_win, s_ctx], axis=-1), axis=-1).astype(v.dtype)
        p_win = p[..., :kr * KB_C].reshape(b, h, ncb, QB_C, kr, KB_C)
        p_ctx = p[..., kr * KB_C:]
        o = jnp.einsum('bhjqrk,brjkhd->bjqhd', p_win, vw) + jnp.einsum('bhjql,blhd->bjqhd', p_ctx, cv)
        return o.reshape(b, GRID_W, h, hd)

    out = lax.map(one_row, (jnp.arange(rows), qg))
    return jnp.moveaxis(out, 0, 1).reshape(b, t, h * hd)


def centred_dwconv(x, w, bias):
    t = x.shape[1]
    left = (CONV_W - 1) // 2
    xp = jnp.pad(x, ((0, 0), (left, CONV_W - 1 - left), (0, 0)))
    return sum(xp[:, i:i + t] * w[i] for i in range(CONV_W)) + bias


def blockdiag(x, w, bias):
    xb = x.reshape(x.shape[0], x.shape[1], H_B, BD_B)
    return jnp.einsum('bthi,hij->bthj', xb, w).reshape(x.shape) + bias


def _lin_combine(left, right):
    a1, b1 = left
    a2, b2 = right
    return a1 * a2, a2 * b1 + b2


def rglru_scan(x, wa, ba, wx, bx, lam, h0, reverse):
    r = jax.nn.sigmoid(blockdiag(x, wa, ba).astype(jnp.float32))
    i = jax.nn.sigmoid(blockdiag(x, wx, bx).astype(jnp.float32))
    log_a = -LRU_C * r * jax.nn.softplus(-lam.astype(jnp.float32))
    a = jnp.exp(log_a)
    u = jnp.sqrt(-jnp.expm1(2.0 * log_a)) * (i * x)
    t0 = -1 if reverse else 0
    u = u.at[:, t0].add(a[:, t0] * h0.astype(jnp.float32))
    _, hs = lax.associative_scan(_lin_combine, (a, u), axis=1, reverse=reverse)
    return hs, hs[:, 0 if reverse else -1]


def rglru_branch(xb, gb, conv_w, conv_b, wa, ba, wx, bx, lam, h0):
    xc = centred_dwconv(xb, conv_w, conv_b).astype(jnp.float32)
    hf, fin_f = rglru_scan(xc, wa[0], ba[0], wx[0], bx[0], lam[0], h0[:, 0], False)
    hb, fin_b = rglru_scan(xc, wa[1], ba[1], wx[1], bx[1], lam[1], h0[:, 1], True)
    out = (hf + hb) * jax.nn.gelu(gb.astype(jnp.float32))
    return out.astype(xb.dtype), jnp.stack([fin_f, fin_b], axis=1)


def rope_2d(x):
    t, dh = x.shape[1], x.shape[-1]
    half = dh // 2
    nf = half // 2
    pos = jnp.arange(t)
    freqs = ROPE_BASE ** (-jnp.arange(nf, dtype=jnp.float32) / nf)

    def rot(xa, p):
        ang = p.astype(jnp.float32)[:, None] * freqs[None, :]
        cos = jnp.cos(ang)[None, :, None, :]
        sin = jnp.sin(ang)[None, :, None, :]
        x1, x2 = xa[..., :nf], xa[..., nf:]
        return jnp.concatenate([x1 * cos - x2 * sin, x1 * sin + x2 * cos], axis=-1)

    xf = x.astype(jnp.float32)
    return jnp.concatenate([rot(xf[..., :half], pos // GRID_W), rot(xf[..., half:], pos % GRID_W)], axis=-1)


def mlstm_scan(q, k, v, ig, lf, c0, n0, m0):
    b, t, h, _ = q.shape
    nc = t // CHUNK

    def to_chunks(a):
        return jnp.moveaxis(a.reshape(b, nc, CHUNK, *a.shape[2:]), 1, 0)

    causal = np.tril(np.ones((CHUNK, CHUNK), dtype=bool))[None, :, :, None]

    def step(carry, inp):
        cm, nv, m = carry
        qq, kk, vv, ii, ff = inp
        bc = jnp.cumsum(ff, axis=1)
        dmat = bc[:, :, None, :] - bc[:, None, :, :] + ii[:, None, :, :]
        dmat = jnp.where(causal, dmat, -jnp.inf)
        inter = bc + m[:, None, :]
        m_t = jnp.maximum(inter, jnp.max(dmat, axis=2))
        w = jnp.exp(dmat - m_t[:, :, None, :])
        g = jnp.exp(inter - m_t)
        s = jnp.einsum('bthd,bshd->btsh', qq, kk) * w
        num = jnp.einsum('btsh,bshe->bthe', s, vv) + g[..., None] * jnp.einsum('bthd,bhde->bthe', qq, cm)
        den = jnp.sum(s, axis=2) + g * jnp.einsum('bthd,bhd->bth', qq, nv)
        hout = num / jnp.maximum(jnp.abs(den), jnp.exp(-m_t))[..., None]
        bk = bc[:, -1]
        dend = bk[:, None, :] - bc + ii
        m_new = jnp.maximum(bk + m, jnp.max(dend, axis=1))
        we = jnp.exp(dend - m_new[:, None, :])
        ge = jnp.exp(bk + m - m_new)
        c_new = ge[..., None, None] * cm + jnp.einsum('bsh,bshd,bshe->bhde', we, kk, vv)
        n_new = ge[..., None] * nv + jnp.einsum('bsh,bshd->bhd', we, kk)
        return (c_new, n_new, m_new), hout

    (cf, nf, mf), hs = lax.scan(step, (c0, n0, m0), (to_chunks(q), to_chunks(k), to_chunks(v), to_chunks(ig), to_chunks(lf)))
    return jnp.moveaxis(hs, 0, 1).reshape(b, t, h, v.shape[-1]), cf, nf, mf


def mlstm_mixer(hin, w_in, b_gate, mh_g, w_out, c0, n0, m0, rotary):
    b, t, _ = hin.shape
    f32 = jnp.float32
    qk = H_C * DK_C
    z = hin @ w_in
    q, k, v, o, gates = jnp.split(z, [qk, 2 * qk, 2 * qk + D_C, 2 * qk + 2 * D_C], axis=-1)
    q = q.reshape(b, t, H_C, DK_C).astype(f32)
    k = k.reshape(b, t, H_C, DK_C).astype(f32) * (DK_C ** -0.5)
    v = v.reshape(b, t, H_C, DV_C).astype(f32)
    if rotary:
        q = rope_2d(q)
        k = rope_2d(k)
    gates = gates.reshape(b, t, 4, H_C).astype(f32) + b_gate.astype(f32)
    ig_f, lf_f = gates[:, :, 0], jax.nn.log_sigmoid(gates[:, :, 1])
    ig_b, lf_b = gates[:, :, 2], jax.nn.log_sigmoid(gates[:, :, 3])
    c0 = c0.astype(f32)
    n0 = n0.astype(f32)
    m0 = m0.astype(f32)
    hf, cf, nf, mf = mlstm_scan(q, k, v, ig_f, lf_f, c0[:, 0], n0[:, 0], m0[:, 0])
    fl = lambda a: jnp.flip(a, axis=1)
    hb, cb, nb, mb = mlstm_scan(fl(q), fl(k), fl(v), fl(ig_b), fl(lf_b), c0[:, 1], n0[:, 1], m0[:, 1])
    hsum = hf + fl(hb)
    hn = rms_norm(hsum, mh_g) * jax.nn.sigmoid(o.reshape(b, t, H_C, DV_C).astype(f32))
    y = hn.reshape(b, t, D_C).astype(hin.dtype) @ w_out
    return y, jnp.stack([cf, cb], axis=1), jnp.stack([nf, nb], axis=1), jnp.stack([mf, mb], axis=1)


def setup_inputs(seed: int = 0) -> dict:
    key = jax.random.key(seed)
    ks = iter(jax.random.split(key, 40))
    f32 = jnp.float32

    def nrm(shape, s):
        return jax.random.normal(next(ks), shape, f32) * s

    x_prompt = nrm((BATCH, SEQ, D_MODEL), 1.0)
    x_sample = nrm((DEC_BATCH, DEC_SEQ, D_MODEL), 1.0)
    cache_k = nrm((DEC_BATCH, N_EVEN, PAST_LEN, H_A, HD_A), 1.0)
    cache_v = nrm((DEC_BATCH, N_EVEN, PAST_LEN, H_A, HD_A), 1.0)
    state_lru = nrm((DEC_BATCH, N_EVEN, 2, D_RNN), 0.5)
    state_mlstm_c = nrm((DEC_BATCH, N_ODD, 2, H_C, DK_C, DV_C), 0.5)
    state_mlstm_n = nrm((DEC_BATCH, N_ODD, 2, H_C, DK_C), 0.5)
    state_mlstm_m = nrm((DEC_BATCH, N_ODD, 2, H_C), 1.0)
    c = nrm((DEC_BATCH, D_MODEL), 1.0)
    c_ctx = nrm((D_MODEL,), 1.0)
    w_mod = nrm((DEPTH, D_MODEL, N_MOD * D_MODEL), 0.5 * D_MODEL ** -0.5)
    b_mod = nrm((DEPTH, N_MOD * D_MODEL), 0.02)
    norm_g = 1.0 + nrm((DEPTH, 3, D_MODEL), 0.02)
    w_ffn_gate = nrm((DEPTH, 2, D_MODEL, D_FF), D_MODEL ** -0.5)
    w_ffn_up = nrm((DEPTH, 2, D_MODEL, D_FF), D_MODEL ** -0.5)
    w_ffn_down = nrm((DEPTH, 2, D_FF, D_MODEL), D_FF ** -0.5)
    w_in_ab = nrm((N_EVEN, D_MODEL, IN_AB), D_MODEL ** -0.5)
    w_out_ab = nrm((N_EVEN, D_A + D_RNN, D_MODEL), (D_A + D_RNN) ** -0.5)
    qn_g = 1.0 + nrm((N_EVEN, HD_A), 0.02)
    kn_g = 1.0 + nrm((N_EVEN, HD_A), 0.02)
    rpb = nrm((N_EVEN, H_A, 2 * WIN_R - 1, 2 * WIN_C - 1), 0.1)
    conv_w = nrm((N_EVEN, CONV_W, D_RNN), CONV_W ** -0.5)
    conv_b = nrm((N_EVEN, D_RNN), 0.02)
    lru_wa = nrm((N_EVEN, 2, H_B, BD_B, BD_B), BD_B ** -0.5)
    lru_ba = nrm((N_EVEN, 2, D_RNN), 0.02)
    lru_wx = nrm((N_EVEN, 2, H_B, BD_B, BD_B), BD_B ** -0.5)
    lru_bx = nrm((N_EVEN, 2, D_RNN), 0.02)
    u = jax.random.uniform(next(ks), (N_EVEN, 2, D_RNN), f32, 0.9, 0.999)
    a = u ** (1.0 / LRU_C)
    lru_lam = jnp.log(a) - jnp.log1p(-a)
    w_in_c = nrm((N_ODD, D_MODEL, IN_C), D_MODEL ** -0.5)
    b_gate_c = nrm((N_ODD, 4, H_C), 0.1) + jnp.array([-1.0, 3.0, -1.0, 3.0], f32)[None, :, None]
    mh_norm_g = 1.0 + nrm((N_ODD, H_C, DV_C), 0.02)
    w_out_c = nrm((N_ODD, D_C, D_MODEL), D_C ** -0.5)
    return {'x_prompt': x_prompt, 'x_sample': x_sample, 'cache_k': cache_k, 'cache_v': cache_v,
            'state_lru': state_lru, 'state_mlstm_c': state_mlstm_c, 'state_mlstm_n': state_mlstm_n,
            'state_mlstm_m': state_mlstm_m, 'c': c, 'c_ctx': c_ctx, 'w_mod': w_mod, 'b_mod': b_mod,
            'norm_g': norm_g, 'w_ffn_gate': w_ffn_gate, 'w_ffn_up': w_ffn_up, 'w_ffn_down': w_ffn_down,
            'w_in_ab': w_in_ab, 'w_out_ab': w_out_ab, 'qn_g': qn_g, 'kn_g': kn_g, 'rpb': rpb,
            'conv_w': conv_w, 'conv_b': conv_b, 'lru_wa': lru_wa, 'lru_ba': lru_ba, 'lru_wx': lru_wx,
            'lru_bx': lru_bx, 'lru_lam': lru_lam, 'w_in_c': w_in_c, 'b_gate_c': b_gate_c,
            'mh_norm_g': mh_norm_g, 'w_out_c': w_out_c}


def reference(x_prompt, x_sample, cache_k, cache_v, state_lru, state_mlstm_c, state_mlstm_n, state_mlstm_m, c, c_ctx, w_mod, b_mod, norm_g, w_ffn_gate, w_ffn_up, w_ffn_down, w_in_ab, w_out_ab, qn_g, kn_g, rpb, conv_w, conv_b, lru_wa, lru_ba, lru_wx, lru_bx, lru_lam, w_in_c, b_gate_c, mh_norm_g, w_out_c):
    f32 = jnp.float32
    xp, xs = x_prompt, x_sample
    bp = xp.shape[0]
    new_k, new_v, new_lru, new_c, new_n, new_m = [], [], [], [], [], []
    for l in range(DEPTH):
        j = l // 2
        mp = modulation(c_ctx[None, :], w_mod[l], b_mod[l])
        ms = modulation(c, w_mod[l], b_mod[l])
        xp = macaron_ffn(xp, mp, 0, norm_g[l, 0], w_ffn_gate[l, 0], w_ffn_up[l, 0], w_ffn_down[l, 0])
        xs = macaron_ffn(xs, ms, 0, norm_g[l, 0], w_ffn_gate[l, 0], w_ffn_up[l, 0], w_ffn_down[l, 0])
        hp = adaln(xp, norm_g[l, 1], mp, 1)
        hs = adaln(xs, norm_g[l, 1], ms, 1)
        if l % 2 == 0:
            lru_p = (conv_w[j], conv_b[j], lru_wa[j], lru_ba[j], lru_wx[j], lru_bx[j], lru_lam[j])
            qa, ka, va, xb, gb = ab_project(hp, w_in_ab[j], qn_g[j], kn_g[j])
            oa = ctx_attention(qa, ka, va)
            ob, h_fin = rglru_branch(xb, gb, *lru_p, jnp.zeros((bp, 2, D_RNN), f32))
            yp = jnp.concatenate([oa, ob], axis=-1) @ w_out_ab[j]
            new_k.append(ka)
            new_v.append(va)
            new_lru.append(h_fin.astype(xp.dtype))
            qa, ka, va, xb, gb = ab_project(hs, w_in_ab[j], qn_g[j], kn_g[j])
            oa = neighbourhood_attention(qa, ka, va, cache_k[:, j], cache_v[:, j], rpb[j])
            ob, _ = rglru_branch(xb, gb, *lru_p, state_lru[:, j])
            ys = jnp.concatenate([oa, ob], axis=-1) @ w_out_ab[j]
        else:
            zc = jnp.zeros((bp, 2, H_C, DK_C, DV_C), f32)
            zn = jnp.zeros((bp, 2, H_C, DK_C), f32)
            zm = jnp.zeros((bp, 2, H_C), f32)
            yp, cfin, nfin, mfin = mlstm_mixer(hp, w_in_c[j], b_gate_c[j], mh_norm_g[j], w_out_c[j], zc, zn, zm, False)
            new_c.append(cfin.astype(xp.dtype))
            new_n.append(nfin.astype(xp.dtype))
            new_m.append(mfin.astype(xp.dtype))
            ys, _, _, _ = mlstm_mixer(hs, w_in_c[j], b_gate_c[j], mh_norm_g[j], w_out_c[j], state_mlstm_c[:, j], state_mlstm_n[:, j], state_mlstm_m[:, j], True)
        xp = xp + mp[:, 5][:, None] * yp
        xs = xs + ms[:, 5][:, None] * ys
        xp = macaron_ffn(xp, mp, 2, norm_g[l, 2], w_ffn_gate[l, 1], w_ffn_up[l, 1], w_ffn_down[l, 1])
        xs = macaron_ffn(xs, ms, 2, norm_g[l, 2], w_ffn_gate[l, 1], w_ffn_up[l, 1], w_ffn_down[l, 1])
    return (xp, xs, jnp.stack(new_k, axis=1), jnp.stack(new_v, axis=1), jnp.stack(new_lru, axis=1), jnp.stack(new_c, axis=1), jnp.stack(new_n, axis=1), jnp.stack(new_m, axis=1))
```

```python
import contextlib
import numpy as np
import concourse.bass as bass
import concourse.mybir as mybir
from concourse.bass_utils import run_bass_kernel_spmd

F32 = mybir.dt.float32
BF16 = mybir.dt.bfloat16
U8 = mybir.dt.uint8
ALU = mybir.AluOpType
AF = mybir.ActivationFunctionType
AX = mybir.AxisListType

D = 2048
DFF = 5632
KC = 16
FC = 44
TS = 4096
TP = 512
NT = TS + TP
NTILE = NT // 512
EPS = 1e-6
ENGS = ("pe", "act", "dve", "pool", "sp")
DTSIZE = {F32: 4, BF16: 2, U8: 1}


class Buf:
    __slots__ = ("w", "rs")

    def __init__(self):
        self.w = None
        self.rs = []


class Op:
    __slots__ = ("eng", "fn", "deps", "is_dma", "sem", "semval", "sig", "signo", "idx")

    def __init__(self, eng, fn, is_dma):
        self.eng = eng
        self.fn = fn
        self.deps = []
        self.is_dma = is_dma
        self.sem = None
        self.semval = 0
        self.sig = False
        self.signo = 0


class Prog:
    def __init__(self, nc, n_dma_sems=16):
        self.nc = nc
        self.ops = {e: [] for e in ENGS}
        self.n_dma_sems = n_dma_sems
        self.dma_rr = {e: 0 for e in ENGS}
        self.dma_last = {}
        self.pending = {e: [] for e in ENGS}

    def op(self, eng, fn, reads=(), writes=(), dma=False):
        o = Op(eng, fn, dma)
        o.idx = len(self.ops[eng])
        deps = list(self.pending[eng])
        self.pending[eng] = []
        for b in reads:
            if b.w is not None:
                deps.append(b.w)
        for b in writes:
            if b.w is not None:
                deps.append(b.w)
            deps.extend(b.rs)
        if dma:
            slot = self.dma_rr[eng] % self.n_dma_sems
            self.dma_rr[eng] += 1
            o.sem = (eng, slot)
            prev = self.dma_last.get(o.sem)
            if prev is not None:
                deps.append(prev)
            self.dma_last[o.sem] = o
        best = {}
        dmas = {}
        for d in deps:
            if d is o:
                continue
            if d.is_dma:
                dmas[id(d)] = d
            else:
                if d.eng == "pe" and eng == "pe" and not dma:
                    continue
                if d.eng not in best or best[d.eng].idx < d.idx:
                    best[d.eng] = d
        for d in list(best.values()) + list(dmas.values()):
            o.deps.append(d)
            d.sig = True
        for b in reads:
            if not dma:
                b.rs = [r for r in b.rs if r.is_dma or r.eng != eng]
            b.rs.append(o)
        for b in writes:
            b.w = o
            b.rs = []
        self.ops[eng].append(o)
        return o

    def dma(self, eng, out, in_, reads=(), writes=(), **kw):
        return self.op(eng, lambda e: e.dma_start(out=out, in_=in_, **kw), reads, writes, dma=True)

    def barrier(self):
        deps = []
        for e in ENGS:
            last = None
            for o in reversed(self.ops[e]):
                if not o.is_dma:
                    last = o
                    break
            if last is not None:
                deps.append(last)
        deps.extend(self.dma_last.values())
        for e in ENGS:
            self.pending[e] = list(deps)

    def emit(self):
        nc = self.nc
        with contextlib.ExitStack() as st:
            esem = {e: st.enter_context(nc.semaphore("s_" + e)) for e in ENGS}
            dsem = {}
            for e in ENGS:
                for s in range(min(self.n_dma_sems, self.dma_rr[e])):
                    dsem[(e, s)] = st.enter_context(nc.semaphore("d_%s_%d" % (e, s)))
            for e in ENGS:
                cnt = 0
                dcnt = {}
                for o in self.ops[e]:
                    if o.is_dma:
                        dcnt[o.sem] = dcnt.get(o.sem, 0) + 16
                        o.semval = dcnt[o.sem]
                    elif o.sig:
                        cnt += 1
                        o.signo = cnt
            block = st.enter_context(nc.Block())
            engobj = {"pe": nc.tensor, "act": nc.scalar, "dve": nc.vector, "pool": nc.gpsimd, "sp": nc.sync}
            all_dma_last = list(self.dma_last.values())

            def run(e):
                eng = engobj[e]
                known = {}

                def wait_for(d):
                    if d.is_dma:
                        key, val, kk = dsem[d.sem], d.semval, ("d",) + d.sem
                    else:
                        key, val, kk = esem[d.eng], d.signo, ("e", d.eng)
                    if known.get(kk, 0) >= val:
                        return
                    known[kk] = val
                    eng.wait_ge(key, val)

                for o in self.ops[e]:
                    for d in o.deps:
                        wait_for(d)
                    ins = o.fn(eng)
                    if o.is_dma:
                        ins.then_inc(dsem[o.sem], 16)
                    elif o.sig:
                        ins.then_inc(esem[e], 1)
                if e == "sp":
                    for d in all_dma_last:
                        wait_for(d)

            block.tensor(lambda x: run("pe"))
            block.scalar(lambda x: run("act"))
            block.vector(lambda x: run("dve"))
            block.gpsimd(lambda x: run("pool"))
            block.sync(lambda x: run("sp"))


class T:
    __slots__ = ("ap", "b")

    def __init__(self, ap, b=None):
        self.ap = ap
        self.b = b if b is not None else Buf()


def prod(xs):
    r = 1
    for x in xs:
        r *= x
    return r


class KB:
    ARENA = 207 * 1024

    def __init__(self):
        self.nc = bass.Bass("TRN2", target_bir_lowering=False)
        self.P = Prog(self.nc)
        self.arena = self.nc.alloc_sbuf_tensor("arena", [128, self.ARENA], U8)
        self.top = 0
        self.marks = []
        self.psum = self.nc.alloc_psum_tensor("psum", [128, 4096], F32)
        self.pbank = [T(self.psum[:, 512 * i:512 * (i + 1)]) for i in range(8)]
        self.nd = 0

    def tile(self, shape, dt):
        free = prod(shape[1:])
        nbytes = (free * DTSIZE[dt] + 63) // 64 * 64
        assert self.top + nbytes <= self.ARENA, "SBUF arena overflow %d" % (self.top + nbytes)
        ap = self.arena[0:shape[0], self.top:self.top + free * DTSIZE[dt]].bitcast(dt)
        self.top += nbytes
        if len(shape) == 3:
            ap = ap.rearrange("p (a b) -> p a b", a=shape[1])
        elif len(shape) == 4:
            ap = ap.rearrange("p (a b c) -> p a b c", a=shape[1], b=shape[2])
        return T(ap)

    def push(self):
        self.marks.append(self.top)

    def pop(self):
        self.top = self.marks.pop()
        self.P.barrier()

    def dram(self, name, shape, dt, kind="Internal"):
        return self.nc.dram_tensor(name, list(shape), dt, kind=kind).ap()

    def op(self, eng, fn, reads=(), writes=()):
        return self.P.op(eng, fn, [t.b for t in reads], [t.b for t in writes])

    def dma(self, eng, out_ap, in_ap, reads=(), writes=(), **kw):
        return self.P.dma(eng, out_ap, in_ap, [t.b for t in reads], [t.b for t in writes], **kw)


class XTB:
    def __init__(self, ap):
        self.ap = ap
        self.t = [T(ap) for _ in range(NTILE)]


def bcast_part(ap, n):
    return bass.AP(ap.tensor, ap.offset, [[0, n]] + [list(x) for x in ap.ap[1:]])


def stage_consts(K, ins):
    c = {}
    c["ident_f"] = K.tile([128, 128], F32)
    c["ident_b"] = K.tile([128, 128], BF16)
    c["ones_b"] = K.tile([128, 128], BF16)
    K.dma("sp", c["ident_f"].ap, ins["ident"], writes=[c["ident_f"]])
    K.op("dve", lambda e: e.tensor_copy(out=c["ident_b"].ap, in_=c["ident_f"].ap), reads=[c["ident_f"]], writes=[c["ident_b"]])
    K.op("dve", lambda e: e.memset(c["ones_b"].ap, 1.0), writes=[c["ones_b"]])
    c["modT"] = [K.tile([128, 9, KC, 2], F32) for _ in range(2)]
    c["gs"] = [K.tile([128, 3, KC, 2], F32) for _ in range(2)]
    c["gate"] = [K.tile([128, 3, KC, 2], F32) for _ in range(2)]
    c["normg"] = K.tile([128, 2, 3, KC], F32)
    K.dma("sp", c["normg"].ap, ins["norm_g"].rearrange("l j (k p) -> p l j k", p=128), writes=[c["normg"]],
          allow_slow_non_contiguous=True)
    return c


def stage_precast(K, src, dst, reads=(), eng="pool"):
    t = T(dst)
    K.dma(eng, dst, src, reads=reads, writes=[t])
    return t


def stage_mod(K, C, ins, l):
    K.push()
    condT = K.tile([128, KC, 2], F32)
    condTb = K.tile([128, KC, 2], BF16)
    for c_ in range(2):
        K.dma("sp", condT.ap[:, :, c_], ins["cond"][c_].rearrange("(k p) -> p k", p=128), writes=[condT], allow_slow_non_contiguous=True)
    K.op("act", lambda e: e.activation(out=condTb.ap, in_=condT.ap, func=AF.Silu), reads=[condT], writes=[condTb])
    wt = [K.tile([128, KC, 512], BF16) for _ in range(3)]
    bt = [K.tile([2, 512], F32) for _ in range(2)]
    row = [K.tile([2, 512], F32) for _ in range(2)]
    psT = K.pbank[2]
    NCH = 9 * D // 512
    for n in range(NCH):
        w = wt[n % 3]
        K.dma("pool", w.ap, ins["w_mod"][l, :, n * 512:(n + 1) * 512].rearrange("(k p) f -> p k f", p=128), writes=[w])
        b = bt[n % 2]
        K.dma("sp", b.ap, bcast_part(ins["b_mod"][l:l + 1, n * 512:(n + 1) * 512], 2), writes=[b])
        ps = K.pbank[n % 2]
        for k in range(KC):
            K.op("pe", lambda e, k=k, w=w, ps=ps: e.matmul(ps.ap[0:2, :], condTb.ap[:, k, :], w.ap[:, k, :],
                                                            start=(k == 0), stop=(k == KC - 1)),
                 reads=[condTb, w], writes=[ps])
        r = row[n % 2]
        K.op("dve", lambda e, r=r, ps=ps, b=b: e.tensor_tensor(out=r.ap, in0=ps.ap[0:2, :], in1=b.ap, op=ALU.add),
             reads=[ps, b], writes=[r])
        for i in range(4):
            col = n * 4 + i
            K.op("pe", lambda e, r=r, i=i, col=col: e.matmul(psT.ap[:, 2 * col:2 * col + 2], r.ap[:, i * 128:(i + 1) * 128],
                                                             C["ident_f"].ap[0:2, 0:2], start=True, stop=True),
                 reads=[r, C["ident_f"]], writes=[psT])
    modT = C["modT"][l]
    K.op("dve", lambda e: e.tensor_copy(out=modT.ap.rearrange("p j k c -> p (j k c)"), in_=psT.ap[:, 0:2 * 9 * KC]),
         reads=[psT], writes=[modT])
    gs, gate = C["gs"][l], C["gate"][l]
    for j in range(3):
        for c in range(2):
            K.op("dve", lambda e, j=j, c=c: e.scalar_tensor_tensor(
                out=gs.ap[:, j, :, c], in0=modT.ap[:, 3 * j + 1, :, c], scalar=1.0, in1=C["normg"].ap[:, l, j, :],
                op0=ALU.add, op1=ALU.mult), reads=[modT, C["normg"]], writes=[gs])
        K.op("dve", lambda e, j=j: e.tensor_scalar(out=gate.ap[:, j, :, :], in0=modT.ap[:, 3 * j + 2, :, :],
                                                   scalar1=(1.0 if j == 1 else 0.5), scalar2=None, op0=ALU.mult),
             reads=[modT], writes=[gate])
    K.pop()


def tile_cond(i):
    return 0 if i < TS // 512 else 1


def stage_to_fm(K, C, srcs, XT):
    K.push()
    xin = [K.tile([128, D], F32) for _ in range(2)]
    st = [K.tile([128, KC, 512], F32) for _ in range(2)]
    for i in range(NTILE):
        s = st[i % 2]
        for sub in range(4):
            x = xin[sub % 2]
            K.dma("sp", x.ap, srcs[i * 4 + sub], writes=[x])
            for q in range(4):
                pb = K.pbank[(sub * 4 + q) % 8]
                for r in range(4):
                    k = q * 4 + r
                    K.op("pe", lambda e, pb=pb, r=r, x=x, k=k: e.transpose(pb.ap[:, r * 128:(r + 1) * 128], x.ap[:, k * 128:(k + 1) * 128],
                                                                           C["ident_f"].ap), reads=[x, C["ident_f"]], writes=[pb])
                eng = "act" if q % 2 == 0 else "dve"
                if eng == "act":
                    K.op("act", lambda e, pb=pb, s=s, q=q, sub=sub: e.copy(out=s.ap[:, 4 * q:4 * q + 4, sub * 128:(sub + 1) * 128],
                                                                          in_=pb.ap.rearrange("p (r t) -> p r t", r=4)),
                         reads=[pb], writes=[s])
                else:
                    K.op("dve", lambda e, pb=pb, s=s, q=q, sub=sub: e.tensor_copy(out=s.ap[:, 4 * q:4 * q + 4, sub * 128:(sub + 1) * 128],
                                                                                 in_=pb.ap.rearrange("p (r t) -> p r t", r=4)),
                         reads=[pb], writes=[s])
        K.dma("sp", XT.ap[:, i * 512:(i + 1) * 512].rearrange("(k p) t -> p k t", p=128), s.ap, reads=[s])
    K.pop()


def stage_to_tm(K, C, XT, dsts):
    K.push()
    xin = [K.tile([128, KC, 512], F32) for _ in range(2)]
    st = [K.tile([128, D], F32) for _ in range(2)]
    for i in range(NTILE):
        x = xin[i % 2]
        K.dma("sp", x.ap, XT.ap[:, i * 512:(i + 1) * 512].rearrange("(k p) t -> p k t", p=128), reads=[XT.t[i]], writes=[x])
        for sub in range(4):
            s = st[sub % 2]
            for q in range(4):
                pb = K.pbank[(sub * 4 + q) % 8]
                for r in range(4):
                    k = q * 4 + r
                    K.op("pe", lambda e, pb=pb, r=r, x=x, k=k, sub=sub: e.transpose(pb.ap[:, r * 128:(r + 1) * 128],
                                                                                   x.ap[:, k, sub * 128:(sub + 1) * 128],
                                                                                   C["ident_f"].ap), reads=[x, C["ident_f"]], writes=[pb])
                if q % 2 == 0:
                    K.op("act", lambda e, pb=pb, s=s, q=q: e.copy(out=s.ap[:, q * 512:(q + 1) * 512], in_=pb.ap), reads=[pb], writes=[s])
                else:
                    K.op("dve", lambda e, pb=pb, s=s, q=q: e.tensor_copy(out=s.ap[:, q * 512:(q + 1) * 512], in_=pb.ap), reads=[pb], writes=[s])
            K.dma("sp", dsts[i * 4 + sub], s.ap, reads=[s])
    K.pop()


class FrontEnd:
    def __init__(self, K, C, psb):
        self.K, self.C = K, C
        self.x = K.tile([128, KC, 512], F32)
        self.sq = [K.tile([128, 512], BF16) for _ in range(2)]
        self.rstd = K.tile([128, 512], F32)
        self.tmp = [K.tile([128, 512], F32) for _ in range(2)]
        self.psb = psb

    def load(self, XT, i):
        K = self.K
        for q in range(4):
            K.dma("sp", self.x.ap[:, 4 * q:4 * q + 4, :],
                  XT.ap[q * 512:(q + 1) * 512, i * 512:(i + 1) * 512].rearrange("(k p) t -> p k t", p=128),
                  reads=[XT.t[i]], writes=[self.x])

    def stats(self):
        K, C = self.K, self.C
        for k in range(KC):
            sq = self.sq[k % 2]
            eng = "act" if k % 2 == 0 else "pool"
            if eng == "act":
                K.op("act", lambda e, k=k, sq=sq: e.activation(out=sq.ap, in_=self.x.ap[:, k, :], func=AF.Square),
                     reads=[self.x], writes=[sq])
            else:
                K.op("pool", lambda e, k=k, sq=sq: e.tensor_tensor(out=sq.ap, in0=self.x.ap[:, k, :], in1=self.x.ap[:, k, :], op=ALU.mult),
                     reads=[self.x], writes=[sq])
            K.op("pe", lambda e, k=k, sq=sq: e.matmul(self.psb.ap, C["ones_b"].ap, sq.ap, start=(k == 0), stop=(k == KC - 1)),
                 reads=[sq, C["ones_b"]], writes=[self.psb])
        K.op("dve", lambda e: e.tensor_scalar(out=self.rstd.ap, in0=self.psb.ap, scalar1=1.0 / D, scalar2=EPS, op0=ALU.mult, op1=ALU.add),
             reads=[self.psb], writes=[self.rstd])
        K.op("act", lambda e: e.activation(out=self.rstd.ap, in_=self.rstd.ap, func=AF.Sqrt), reads=[self.rstd], writes=[self.rstd])
        K.op("dve", lambda e: e.reciprocal(out=self.rstd.ap, in_=self.rstd.ap), reads=[self.rstd], writes=[self.rstd])

    def modulate(self, hT, l, j, cond):
        K, C = self.K, self.C
        gs, modT = C["gs"][l], C["modT"][l]
        for k in range(KC):
            t = self.tmp[k % 2]
            K.op("dve", lambda e, k=k, t=t: e.scalar_tensor_tensor(out=t.ap, in0=self.x.ap[:, k, :], scalar=gs.ap[:, j, k, cond:cond + 1],
                                                                    in1=self.rstd.ap, op0=ALU.mult, op1=ALU.mult),
                 reads=[self.x, gs, self.rstd], writes=[t])
            K.op("pool", lambda e, k=k, t=t: e.tensor_scalar(out=hT.ap[:, k, :], in0=t.ap, scalar1=modT.ap[:, 3 * j, k, cond:cond + 1],
                                                             scalar2=None, op0=ALU.add),
                 reads=[t, modT], writes=[hT])


def stage_ffn(K, C, l, jf, XTin, XTout, wgD, wuD, wdD):
    j = 0 if jf == 0 else 2
    K.push()
    fe = FrontEnd(K, C, K.pbank[4])
    hT = K.tile([128, KC, 512], BF16)
    act = K.tile([128, FC, 512], BF16)
    wg = [K.tile([128, KC, 256], BF16) for _ in range(2)]
    wu = [K.tile([128, KC, 256], BF16) for _ in range(2)]
    wd = [K.tile([128, FC, 128], BF16) for _ in range(2)]
    sg = [K.tile([128, 512], F32) for _ in range(2)]
    xres = [K.tile([128, 512], F32) for _ in range(2)]
    xo = [K.tile([128, 512], F32) for _ in range(2)]
    gate = C["gate"][l]
    for i in range(NTILE):
        cond = tile_cond(i)
        fe.load(XTin, i)
        fe.stats()
        fe.modulate(hT, l, j, cond)
        for n in range(FC // 2):
            g_, u_ = wg[n % 2], wu[n % 2]
            K.dma("sp", g_.ap.rearrange("p k f -> p (k f)"), wgD[n].ap, reads=[wgD[n]], writes=[g_])
            K.dma("sp", u_.ap.rearrange("p k f -> p (k f)"), wuD[n].ap, reads=[wuD[n]], writes=[u_])
            for c in range(2):
                f = 2 * n + c
                pg, pu = K.pbank[f % 2], K.pbank[2 + f % 2]
                for k in range(KC):
                    K.op("pe", lambda e, pg=pg, g_=g_, c=c, k=k: e.matmul(pg.ap, g_.ap[:, k, c * 128:(c + 1) * 128], hT.ap[:, k, :],
                                                                           start=(k == 0), stop=(k == KC - 1)),
                         reads=[g_, hT], writes=[pg])
                for k in range(KC):
                    K.op("pe", lambda e, pu=pu, u_=u_, c=c, k=k: e.matmul(pu.ap, u_.ap[:, k, c * 128:(c + 1) * 128], hT.ap[:, k, :],
                                                                           start=(k == 0), stop=(k == KC - 1)),
                         reads=[u_, hT], writes=[pu])
                s_ = sg[f % 2]
                K.op("act", lambda e, s_=s_, pg=pg: e.activation(out=s_.ap, in_=pg.ap, func=AF.Silu), reads=[pg], writes=[s_])
                K.op("dve", lambda e, s_=s_, pu=pu, f=f: e.tensor_tensor(out=act.ap[:, f, :], in0=pu.ap, in1=s_.ap, op=ALU.mult),
                     reads=[pu, s_], writes=[act])
        for dc in range(KC):
            w_ = wd[dc % 2]
            K.dma("sp", w_.ap.rearrange("p c d -> p (c d)"), wdD[dc].ap, reads=[wdD[dc]], writes=[w_])
            xr = xres[dc % 2]
            K.dma("sp", xr.ap, XTin.ap[dc * 128:(dc + 1) * 128, i * 512:(i + 1) * 512], reads=[XTin.t[i]], writes=[xr])
            py = K.pbank[5 + dc % 2]
            for f in range(FC):
                K.op("pe", lambda e, py=py, w_=w_, f=f: e.matmul(py.ap, w_.ap[:, f, :], act.ap[:, f, :], start=(f == 0), stop=(f == FC - 1)),
                     reads=[w_, act], writes=[py])
            o_ = xo[dc % 2]
            K.op("dve", lambda e, o_=o_, py=py, xr=xr, dc=dc, cond=cond: e.scalar_tensor_tensor(
                out=o_.ap, in0=py.ap, scalar=gate.ap[:, j, dc, cond:cond + 1], in1=xr.ap, op0=ALU.mult, op1=ALU.add),
                reads=[py, xr, gate], writes=[o_])
            K.dma("sp", XTout.ap[dc * 128:(dc + 1) * 128, i * 512:(i + 1) * 512], o_.ap, reads=[o_])
    K.pop()


def precast_ffn(K, ins, l, jf):
    nc = K.nc
    wgD, wuD, wdD = [], [], []
    for name, lst in (("w_ffn_gate", wgD), ("w_ffn_up", wuD)):
        dst = K.dram("%s_bf_%d_%d" % (name, l, jf), [FC // 2, 128, KC * 256], BF16)
        for n in range(FC // 2):
            src = ins[name][l, jf, :, n * 256:(n + 1) * 256].rearrange("(k p) f -> p k f", p=128)
            lst.append(stage_precast(K, src, dst[n].rearrange("p (k f) -> p k f", k=KC)))
            lst[-1].ap = dst[n]
    dst = K.dram("w_ffn_down_bf_%d_%d" % (l, jf), [KC, 128, FC * 128], BF16)
    for dc in range(KC):
        src = ins["w_ffn_down"][l, jf, :, dc * 128:(dc + 1) * 128].rearrange("(c p) d -> p c d", p=128)
        wdD.append(stage_precast(K, src, dst[dc].rearrange("p (c d) -> p c d", c=FC)))
        wdD[-1].ap = dst[dc]
    return wgD, wuD, wdD


HA = 8
LCTX = 512
NEG = -30000.0


def precast_cols(K, w2d, name, ncols, group=256):
    ng = ncols // group
    dst = K.dram(name, [ng, 128, KC * group], BF16)
    out = []
    for n in range(ng):
        src = w2d[:, n * group:(n + 1) * group].rearrange("(k p) f -> p k f", p=128)
        t = stage_precast(K, src, dst[n].rearrange("p (k f) -> p k f", k=KC))
        t.ap = dst[n]
        out.append(t)
    return out


def precast_wout(K, w2d, name):
    dst = K.dram(name, [128, KC * KC * 128], BF16)
    t = T(dst)
    d4 = dst.rearrange("p (c k d) -> p c k d", c=KC, k=KC)
    for dc in range(KC):
        src = w2d[:, dc * 128:(dc + 1) * 128].rearrange("(k p) d -> p k d", p=128)
        K.dma("pool", d4[:, dc, :, :], src, writes=[t])
    return t


def rms_head(K, C, ps, gcol, out_bf, out_f32, sqb, ps2, rt, hd):
    K.op("act", lambda e: e.activation(out=sqb.ap, in_=ps.ap, func=AF.Square), reads=[ps], writes=[sqb])
    K.op("pe", lambda e: e.matmul(ps2.ap, C["ones_b"].ap, sqb.ap, start=True, stop=True), reads=[sqb, C["ones_b"]], writes=[ps2])
    K.op("dve", lambda e: e.tensor_scalar(out=rt.ap, in0=ps2.ap, scalar1=1.0 / hd, scalar2=EPS, op0=ALU.mult, op1=ALU.add),
         reads=[ps2], writes=[rt])
    K.op("act", lambda e: e.activation(out=rt.ap, in_=rt.ap, func=AF.Sqrt), reads=[rt], writes=[rt])
    K.op("dve", lambda e: e.reciprocal(out=rt.ap, in_=rt.ap), reads=[rt], writes=[rt])
    if out_f32 is not None:
        K.op("dve", lambda e: e.scalar_tensor_tensor(out=out_f32.ap, in0=ps.ap, scalar=gcol, in1=rt.ap, op0=ALU.mult, op1=ALU.mult),
             reads=[ps, rt], writes=[out_f32])
        K.op("pool", lambda e: e.tensor_copy(out=out_bf.ap, in_=out_f32.ap), reads=[out_f32], writes=[out_bf])
    else:
        K.op("dve", lambda e: e.scalar_tensor_tensor(out=out_bf.ap, in0=ps.ap, scalar=gcol, in1=rt.ap, op0=ALU.mult, op1=ALU.mult),
             reads=[ps, rt], writes=[out_bf])


def stage_inproj_ab(K, C, ins, outs, XT, S, wD):
    l, j = 0, 1
    K.push()
    fe = FrontEnd(K, C, K.pbank[4])
    hT = K.tile([128, KC, 512], BF16)
    wb = [K.tile([128, KC, 256], BF16) for _ in range(3)]
    gq = K.tile([128, 2], F32)
    K.dma("sp", gq.ap[:, 0:1], ins["qn_g"].rearrange("o p -> p o"), writes=[gq], allow_slow_non_contiguous=True)
    K.dma("sp", gq.ap[:, 1:2], ins["kn_g"].rearrange("o p -> p o"), writes=[gq], allow_slow_non_contiguous=True)
    K.op("dve", lambda e: e.tensor_scalar(out=gq.ap[:, 0:1], in0=gq.ap[:, 0:1], scalar1=128.0 ** -0.5, scalar2=None, op0=ALU.mult),
         reads=[gq], writes=[gq])
    sqb = [K.tile([128, 512], BF16) for _ in range(2)]
    rt = [K.tile([128, 512], F32) for _ in range(2)]
    ob = [K.tile([128, 512], BF16) for _ in range(2)]
    of = [K.tile([128, 512], F32) for _ in range(2)]
    tk = [K.tile([128, 512], F32) for _ in range(2)]
    vb = [K.tile([128, 256], BF16) for _ in range(2)]
    vf = [K.tile([128, 256], F32) for _ in range(2)]
    g1 = [K.tile([128, 512], F32) for _ in range(2)]
    g2 = [K.tile([128, 512], F32) for _ in range(2)]
    cnt = 0
    import os
    dbg_tiles = [int(x) for x in os.environ.get("INPROJ_TILES", "0,1,2,3,4,5,6,7,8").split(",")]
    dbg_segs = [int(x) for x in os.environ.get("INPROJ_SEGS", "0,1,2,3,4").split(",")]
    for i in range(NTILE):
        if i not in dbg_tiles:
            continue
        cond = tile_cond(i)
        prompt = (i == NTILE - 1)
        tok = slice(i * 512, (i + 1) * 512)
        fe.load(XT, i)
        fe.stats()
        fe.modulate(hT, l, j, cond)
        for n in range(20):
            if n // 4 not in dbg_segs:
                continue
            w = wb[n % 3]
            K.dma("sp", w.ap.rearrange("p k f -> p (k f)"), wD[n].ap, reads=[wD[n]], writes=[w])
            seg = n // 4
            if seg == 2:
                for blk in range(4):
                    ps = K.pbank[5 + blk % 2]
                    for k in range(KC):
                        K.op("pe", lambda e, ps=ps, k=k, blk=blk, w=w: e.matmul(ps.ap[:, 0:256], hT.ap[:, k, blk * 128:(blk + 1) * 128], w.ap[:, k, :],
                                                                               start=(k == 0), stop=(k == KC - 1)), reads=[hT, w], writes=[ps])
                    v_ = vb[cnt % 2]
                    cols = slice((n - 8) * 256, (n - 7) * 256)
                    K.op("act", lambda e, v_=v_, ps=ps: e.copy(out=v_.ap, in_=ps.ap[:, 0:256]), reads=[ps], writes=[v_])
                    K.dma("sp", S["V"].ap[i * 512 + blk * 128:i * 512 + (blk + 1) * 128, cols], v_.ap, reads=[v_])
                    cnt += 1
                if prompt:
                    for c in range(2):
                        ch = (n % 4) * 2 + c
                        ps = K.pbank[cnt % 2]
                        for k in range(KC):
                            K.op("pe", lambda e, ps=ps, k=k, c=c, w=w: e.matmul(ps.ap, w.ap[:, k, c * 128:(c + 1) * 128], hT.ap[:, k, :],
                                                                               start=(k == 0), stop=(k == KC - 1)), reads=[hT, w], writes=[ps])
                        f_ = of[cnt % 2]
                        K.op("act", lambda e, f_=f_, ps=ps: e.copy(out=f_.ap, in_=ps.ap), reads=[ps], writes=[f_])
                        pt = K.pbank[7]
                        for blk in range(4):
                            K.op("pe", lambda e, pt=pt, f_=f_, blk=blk: e.transpose(pt.ap[:, blk * 128:(blk + 1) * 128], f_.ap[:, blk * 128:(blk + 1) * 128],
                                                                                   C["ident_f"].ap), reads=[f_, C["ident_f"]], writes=[pt])
                        t_ = tk[ch % 2]
                        K.op("act", lambda e, t_=t_, pt=pt: e.copy(out=t_.ap, in_=pt.ap), reads=[pt], writes=[t_])
                        K.dma("sp", outs["nv"].rearrange("(b p) (h d) -> p b h d", p=128, d=128)[:, :, ch, :],
                              t_.ap.rearrange("p (b d) -> p b d", b=4), reads=[t_])
                        cnt += 1
                continue
            for c in range(2):
                ch = (n % 4) * 2 + c
                ps = K.pbank[cnt % 2]
                for k in range(KC):
                    K.op("pe", lambda e, ps=ps, k=k, c=c, w=w: e.matmul(ps.ap, w.ap[:, k, c * 128:(c + 1) * 128], hT.ap[:, k, :],
                                                                       start=(k == 0), stop=(k == KC - 1)), reads=[hT, w], writes=[ps])
                if seg in (0, 1):
                    o_, f_ = ob[cnt % 2], (of[cnt % 2] if (seg == 1 and prompt) else None)
                    rms_head(K, C, ps, gq.ap[:, seg:seg + 1], o_, f_, sqb[cnt % 2], K.pbank[2 + cnt % 2], rt[cnt % 2], 128.0)
                    dst = S["QT"] if seg == 0 else S["KT"]
                    K.dma("sp", dst.ap[ch, :, tok], o_.ap, reads=[o_])
                    if f_ is not None:
                        pt = K.pbank[7]
                        for blk in range(4):
                            K.op("pe", lambda e, pt=pt, f_=f_, blk=blk: e.transpose(pt.ap[:, blk * 128:(blk + 1) * 128], f_.ap[:, blk * 128:(blk + 1) * 128],
                                                                                   C["ident_f"].ap), reads=[f_, C["ident_f"]], writes=[pt])
                        t_ = tk[ch % 2]
                        K.op("act", lambda e, t_=t_, pt=pt: e.copy(out=t_.ap, in_=pt.ap), reads=[pt], writes=[t_])
                        K.dma("sp", outs["nk"].rearrange("(b p) (h d) -> p b h d", p=128, d=128)[:, :, ch, :],
                              t_.ap.rearrange("p (b d) -> p b d", b=4), reads=[t_])
                elif seg == 3:
                    f_ = of[cnt % 2]
                    K.op("act", lambda e, f_=f_, ps=ps: e.copy(out=f_.ap, in_=ps.ap), reads=[ps], writes=[f_])
                    K.dma("sp", S["XB"].ap[ch, :, tok], f_.ap, reads=[f_])
                else:
                    a_, b_ = g1[cnt % 2], g2[cnt % 2]
                    K.op("act", lambda e, a_=a_, ps=ps: e.activation(out=a_.ap, in_=ps.ap, func=AF.Square), reads=[ps], writes=[a_])
                    K.op("dve", lambda e, a_=a_: e.tensor_scalar(out=a_.ap, in0=a_.ap, scalar1=0.044715, scalar2=1.0, op0=ALU.mult, op1=ALU.add),
                         reads=[a_], writes=[a_])
                    K.op("dve", lambda e, a_=a_, ps=ps: e.tensor_tensor(out=a_.ap, in0=ps.ap, in1=a_.ap, op=ALU.mult), reads=[ps, a_], writes=[a_])
                    K.op("act", lambda e, a_=a_, b_=b_: e.activation(out=b_.ap, in_=a_.ap, func=AF.Sigmoid, scale=1.5957691216057308),
                         reads=[a_], writes=[b_])
                    K.op("dve", lambda e, b_=b_, ps=ps: e.tensor_tensor(out=b_.ap, in0=ps.ap, in1=b_.ap, op=ALU.mult), reads=[ps, b_], writes=[b_])
                    K.dma("sp", S["GG"].ap[ch, :, tok], b_.ap, reads=[b_])
                cnt += 1
    K.pop()


def na_geometry():
    geo = []
    for g in range(32):
        r0a = min(max(2 * g - 4, 0), 56)
        r0b = min(max(2 * g + 1 - 4, 0), 56)
        clo, chi = r0a // 2, (r0b + 7) // 2
        ns = chi - clo + 1
        dmax = chi - g
        typ = {0: 1, 1: 2, 30: 3, 31: 4}.get(g, 0)
        geo.append((chi, ns, 6 - 2 * dmax, typ))
    return geo


def na_host_consts():
    shift = np.zeros((31, 64, 128), np.float32)
    for qc in range(64):
        for p in range(128):
            dc = (p % 64) - qc + 15
            if 0 <= dc <= 30:
                shift[dc, qc, p] = 1.0
    geo = na_geometry()
    mask = np.full((5, 128, 10, 64), NEG, np.float32)
    reps = {0: 10, 1: 0, 2: 1, 3: 30, 4: 31}
    for typ, g in reps.items():
        chi, ns, off, _ = geo[g]
        for i in range(ns):
            for qr2 in range(2):
                r = 2 * g + qr2
                r0 = min(max(r - 4, 0), 56)
                for kr2 in range(2):
                    kr = 2 * (chi - i) + kr2
                    if not (r0 <= kr < r0 + 8):
                        continue
                    for qc in range(64):
                        c0 = min(max(qc - 8, 0), 48)
                        mask[typ, kr2 * 64 + c0:kr2 * 64 + c0 + 16, 2 * i + qr2, qc] = 0.0
    return shift, mask.reshape(5, 128, 640)


def stage_attention(K, C, ins, S):
    geo = na_geometry()
    K.push()
    Ybf = K.tile([128, HA, 14, 64], BF16)
    maskb = K.tile([128, 5, 640], BF16)
    K.dma("pool", maskb.ap, ins["na_mask"].rearrange("t p f -> p t f"), writes=[maskb])
    ckT = K.tile([128, HA, LCTX], BF16)
    cvb = K.tile([128, 4, 1024], BF16)
    K.dma("pool", cvb.ap, ins["cache_v"].rearrange("(c p) f -> p c f", p=128), writes=[cvb])
    K.P.barrier()
    K.push()
    X = K.tile([128, 64, 120], F32)
    sh = K.tile([31, 64, 128], F32)
    rp = K.tile([31, 120], F32)
    K.dma("sp", sh.ap, ins["na_shift"], writes=[sh])
    K.dma("sp", rp.ap, ins["rpbT"], writes=[rp])
    for q4 in range(16):
        pb = K.pbank[q4 % 8]
        for r in range(4):
            qc = q4 * 4 + r
            K.op("pe", lambda e, pb=pb, r=r, qc=qc: e.matmul(pb.ap[:, r * 128:r * 128 + 120], sh.ap[:, qc, :], rp.ap, start=True, stop=True),
                 reads=[sh, rp], writes=[pb])
        K.op("dve", lambda e, pb=pb, q4=q4: e.tensor_copy(out=X.ap[:, q4 * 4:q4 * 4 + 4, :], in_=pb.ap.rearrange("p (r c) -> p r c", r=4)[:, :, 0:120]),
             reads=[pb], writes=[X])
    for h in range(HA):
        for half in range(2):
            ps_ = slice(half * 64, half * 64 + 64)
            j0 = h * 15 + 1 - half
            eng = "dve" if half == 0 else "pool"
            K.op(eng, lambda e, h=h, ps_=ps_, j0=j0: e.tensor_copy(out=Ybf.ap[ps_, h, :, :], in_=X.ap[ps_, :, j0:j0 + 14].rearrange("p q j -> p j q")),
                 reads=[X], writes=[Ybf])
    ck = K.tile([128, 4, 1024], F32)
    K.dma("sp", ck.ap, ins["cache_k"].rearrange("(c p) f -> p c f", p=128), writes=[ck])
    for h in range(HA):
        pb = K.pbank[h % 8]
        for c in range(4):
            K.op("pe", lambda e, pb=pb, c=c, h=h: e.transpose(pb.ap[:, c * 128:(c + 1) * 128], ck.ap[:, c, h * 128:(h + 1) * 128], C["ident_f"].ap),
                 reads=[ck, C["ident_f"]], writes=[pb])
        K.op("act", lambda e, pb=pb, h=h: e.copy(out=ckT.ap[:, h, :], in_=pb.ap), reads=[pb], writes=[ckT])
    K.pop()
    Vs = K.tile([128, 32, 1024], BF16)
    K.dma("sp", Vs.ap, S["V"].ap[0:TS, :].rearrange("(c p) f -> p c f", p=128), writes=[Vs])
    Vp = K.tile([128, 4, 1024], BF16)
    K.dma("sp", Vp.ap, S["V"].ap[TS:NT, :].rearrange("(c p) f -> p c f", p=128), writes=[Vp])
    qT = [K.tile([128, NT], BF16) for _ in range(2)]
    kT = [K.tile([128, NT], BF16) for _ in range(2)]
    Pb = [K.tile([128, 9, 128], BF16) for _ in range(2)]
    rd = [K.tile([128, 256], F32) for _ in range(2)]
    ost = [K.tile([128, 512], BF16) for _ in range(2)]
    it = 0
    for h in range(HA):
        q_, k_ = qT[h % 2], kT[h % 2]
        K.dma("sp", q_.ap, S["QT"].ap[h], writes=[q_])
        K.dma("sp", k_.ap, S["KT"].ap[h], writes=[k_])
        hc = slice(h * 128, (h + 1) * 128)
        for g in range(32):
            chi, ns, off, typ = geo[g]
            nsl = ns + 4
            sreg = T(K.psum[:, (it % 2) * 1536:(it % 2) * 1536 + 1152].rearrange("p (s q) -> p s q", s=9))
            sb_ = [K.pbank[(it % 2) * 3 + x] for x in range(3)]
            qs = q_.ap[:, g * 128:(g + 1) * 128]
            for i in range(ns):
                c = chi - i
                K.op("pe", lambda e, sreg=sreg, i=i, c=c, k_=k_, qs=qs: e.matmul(sreg.ap[:, i, :], k_.ap[:, c * 128:(c + 1) * 128], qs, start=True, stop=False),
                     reads=[k_, q_], writes=sb_)
                K.op("pe", lambda e, sreg=sreg, i=i, h=h, off=off: e.matmul(sreg.ap[:, i, :], C["ident_b"].ap,
                                                                          Ybf.ap[:, h, off + 2 * i:off + 2 * i + 2, :].rearrange("p a q -> p (a q)"),
                                                                          start=False, stop=False), reads=[Ybf, C["ident_b"]], writes=sb_)
                K.op("pe", lambda e, sreg=sreg, i=i, typ=typ: e.matmul(sreg.ap[:, i, :], C["ident_b"].ap, maskb.ap[:, typ, i * 128:(i + 1) * 128],
                                                                     start=False, stop=True), reads=[maskb, C["ident_b"]], writes=sb_)
            for cc in range(4):
                K.op("pe", lambda e, sreg=sreg, cc=cc, ns=ns, h=h, qs=qs: e.matmul(sreg.ap[:, ns + cc, :], ckT.ap[:, h, cc * 128:(cc + 1) * 128], qs,
                                                                                 start=True, stop=True), reads=[ckT, q_], writes=sb_)
            P_ = Pb[it % 2]
            K.op("act", lambda e, P_=P_, sreg=sreg, nsl=nsl: e.activation(out=P_.ap[:, 0:nsl, :], in_=sreg.ap[:, 0:nsl, :], func=AF.Exp),
                 reads=sb_, writes=[P_])
            po = K.pbank[6 + it % 2]
            for i in range(nsl):
                if i < ns:
                    vv = Vs.ap[:, chi - i, hc]
                else:
                    vv = cvb.ap[:, i - ns, hc]
                K.op("pe", lambda e, po=po, vv=vv, P_=P_, i=i, nsl=nsl: e.matmul(po.ap[:, 0:128], vv, P_.ap[:, i, :], start=(i == 0), stop=(i == nsl - 1)),
                     reads=[Vs, cvb, P_], writes=[po])
            for i in range(nsl):
                K.op("pe", lambda e, po=po, P_=P_, i=i, nsl=nsl: e.matmul(po.ap[:, 128:256], C["ones_b"].ap, P_.ap[:, i, :], start=(i == 0), stop=(i == nsl - 1)),
                     reads=[C["ones_b"], P_], writes=[po])
            r_ = rd[it % 2]
            o_ = ost[(g // 4) % 2]
            K.op("dve", lambda e, r_=r_, po=po: e.reciprocal(out=r_.ap[:, 0:128], in_=po.ap[:, 128:256]), reads=[po], writes=[r_])
            K.op("dve", lambda e, r_=r_, po=po, o_=o_, g=g: e.tensor_tensor(out=o_.ap[:, (g % 4) * 128:(g % 4 + 1) * 128], in0=po.ap[:, 0:128], in1=r_.ap[:, 0:128],
                                                                             op=ALU.mult), reads=[po, r_], writes=[o_])
            if g % 4 == 3:
                K.dma("sp", S["OT"].ap[h, :, (g // 4) * 512:(g // 4 + 1) * 512], o_.ap, reads=[o_])
            it += 1
        for s_ in range(2):
            t0 = TS + s_ * 256
            sreg = T(K.psum[:, (it % 2) * 1536:(it % 2) * 1536 + 512].rearrange("p (s q) -> p s q", s=2))
            sb_ = [K.pbank[(it % 2) * 3 + x] for x in range(3)]
            qs = q_.ap[:, t0:t0 + 256]
            for c in range(2):
                K.op("pe", lambda e, sreg=sreg, c=c, k_=k_, qs=qs, t0=t0: e.matmul(sreg.ap[:, c, :], k_.ap[:, t0 + c * 128:t0 + (c + 1) * 128], qs, start=True, stop=True),
                     reads=[k_, q_], writes=sb_)
            P_ = Pb[it % 2]
            Pv = P_.ap.rearrange("p s q -> p (s q)")[:, 0:512].rearrange("p (s q) -> p s q", s=2)
            K.op("act", lambda e, Pv=Pv, sreg=sreg: e.activation(out=Pv, in_=sreg.ap, func=AF.Exp), reads=sb_, writes=[P_])
            po = K.pbank[6 + it % 2]
            for c in range(2):
                K.op("pe", lambda e, po=po, P_=P_, c=c, Pv=Pv, s_=s_, hc=hc: e.matmul(po.ap[:, 0:256], Vp.ap[:, 2 * s_ + c, hc], Pv[:, c, :], start=(c == 0), stop=(c == 1)),
                     reads=[Vp, P_], writes=[po])
            for c in range(2):
                K.op("pe", lambda e, po=po, c=c, Pv=Pv: e.matmul(po.ap[:, 256:512], C["ones_b"].ap, Pv[:, c, :], start=(c == 0), stop=(c == 1)),
                     reads=[C["ones_b"], P_], writes=[po])
            r_ = rd[it % 2]
            o_ = ost[it % 2]
            K.op("dve", lambda e, r_=r_, po=po: e.reciprocal(out=r_.ap, in_=po.ap[:, 256:512]), reads=[po], writes=[r_])
            K.op("dve", lambda e, r_=r_, po=po, o_=o_: e.tensor_tensor(out=o_.ap[:, 0:256], in0=po.ap[:, 0:256], in1=r_.ap, op=ALU.mult),
                 reads=[po, r_], writes=[o_])
            K.dma("sp", S["OT"].ap[h, :, t0:t0 + 256], o_.ap[:, 0:256], reads=[o_])
            it += 1
    K.pop()


def stage_lru(K, C, ins, outs, S):
    K.push()
    segs = [(0, TS, 0), (TS, 256, 1), (TS + 256, 256, 2)]
    TM = TS
    cw = K.tile([128, 4, 8], F32)
    cb = K.tile([128, 8], F32)
    bab = K.tile([128, 2, 2, 8], F32)
    cl = K.tile([128, 2, 8], F32)
    h0 = K.tile([128, 2, 8], F32)
    zero = K.tile([128, 1], F32)
    fin = K.tile([128, 2, 2, 8], F32)
    K.op("dve", lambda e: e.memset(zero.ap, 0.0), writes=[zero])
    for i_ in range(4):
        K.dma("sp", cw.ap[:, i_, :], ins["conv_w"][i_].rearrange("(c p) -> p c", p=128), writes=[cw], allow_slow_non_contiguous=True)
    K.dma("sp", cb.ap, ins["conv_b"].rearrange("(c p) -> p c", p=128), writes=[cb], allow_slow_non_contiguous=True)
    for d_ in range(2):
        K.dma("sp", bab.ap[:, 0, d_, :], ins["lru_ba"][d_].rearrange("(c p) -> p c", p=128), writes=[bab], allow_slow_non_contiguous=True)
        K.dma("sp", bab.ap[:, 1, d_, :], ins["lru_bx"][d_].rearrange("(c p) -> p c", p=128), writes=[bab], allow_slow_non_contiguous=True)
        K.dma("sp", cl.ap[:, d_, :], ins["lru_lam"][d_].rearrange("(c p) -> p c", p=128), writes=[cl], allow_slow_non_contiguous=True)
        K.dma("sp", h0.ap[:, d_, :], ins["state_lru"][d_].rearrange("(c p) -> p c", p=128), writes=[h0], allow_slow_non_contiguous=True)
    K.op("act", lambda e: e.activation(out=cl.ap, in_=cl.ap, func=AF.Exp, scale=-1.0), reads=[cl], writes=[cl])
    K.op("dve", lambda e: e.tensor_scalar(out=cl.ap, in0=cl.ap, scalar1=1.0, scalar2=None, op0=ALU.add), reads=[cl], writes=[cl])
    K.op("act", lambda e: e.activation(out=cl.ap, in_=cl.ap, func=AF.Ln), reads=[cl], writes=[cl])
    K.op("dve", lambda e: e.tensor_scalar(out=cl.ap, in0=cl.ap, scalar1=-8.0, scalar2=None, op0=ALU.mult), reads=[cl], writes=[cl])
    wab = K.tile([128, 2, 2, 8, 128], BF16) if False else None
    wa = K.tile([128, 2 * 2 * 8, 128], BF16)
    for d_ in range(2):
        K.dma("pool", wa.ap[:, (0 * 2 + d_) * 8:(0 * 2 + d_) * 8 + 8, :], ins["lru_wa"][d_].rearrange("h i o -> i h o"), writes=[wa])
        K.dma("pool", wa.ap[:, (1 * 2 + d_) * 8:(1 * 2 + d_) * 8 + 8, :], ins["lru_wx"][d_].rearrange("h i o -> i h o"), writes=[wa])
    K.P.barrier()
    xpad = K.tile([128, TM + 3], F32)
    xc = K.tile([128, TM], F32)
    xcb = K.tile([128, TM], BF16)
    gg = K.tile([128, TM], F32)
    ra = K.tile([128, TM], F32)
    ri = K.tile([128, TM], F32)
    sq = K.tile([128, TM], F32)
    hs = [K.tile([128, TM], F32) for _ in range(2)]
    ob = K.tile([128, TM], BF16)
    for (t0, Tn, sid) in segs:
        for ch in range(8):
            K.op("pool", lambda e: e.memset(xpad.ap[:, 0:1], 0.0), writes=[xpad])
            K.op("pool", lambda e, Tn=Tn: e.memset(xpad.ap[:, Tn + 1:Tn + 3], 0.0), writes=[xpad])
            K.dma("sp", xpad.ap[:, 1:Tn + 1], S["XB"].ap[ch, :, t0:t0 + Tn], writes=[xpad])
            K.dma("sp", gg.ap[:, 0:Tn], S["GG"].ap[ch, :, t0:t0 + Tn], writes=[gg])
            K.op("dve", lambda e, Tn=Tn, ch=ch: e.tensor_scalar(out=xc.ap[:, 0:Tn], in0=xpad.ap[:, 0:Tn], scalar1=cw.ap[:, 0, ch:ch + 1], scalar2=cb.ap[:, ch:ch + 1],
                                                               op0=ALU.mult, op1=ALU.add), reads=[xpad, cw, cb], writes=[xc])
            for i_ in range(1, 4):
                eng = "dve"
                K.op(eng, lambda e, Tn=Tn, ch=ch, i_=i_: e.scalar_tensor_tensor(out=xc.ap[:, 0:Tn], in0=xpad.ap[:, i_:i_ + Tn], scalar=cw.ap[:, i_, ch:ch + 1],
                                                                              in1=xc.ap[:, 0:Tn], op0=ALU.mult, op1=ALU.add), reads=[xpad, cw, xc], writes=[xc])
            K.op("act", lambda e, Tn=Tn: e.copy(out=xcb.ap[:, 0:Tn], in_=xc.ap[:, 0:Tn]), reads=[xc], writes=[xcb])
            for d_ in range(2):
                pw = min(512, Tn)
                for pc in range(Tn // pw):
                    sl = slice(pc * pw, (pc + 1) * pw)
                    pa, px = K.pbank[(2 * pc) % 8], K.pbank[(2 * pc + 1) % 8]
                    K.op("pe", lambda e, pa=pa, sl=sl, d_=d_, ch=ch, pw=pw: e.matmul(pa.ap[:, 0:pw], wa.ap[:, (0 * 2 + d_) * 8 + ch, :], xcb.ap[:, sl], start=True, stop=True),
                         reads=[wa, xcb], writes=[pa])
                    K.op("pe", lambda e, px=px, sl=sl, d_=d_, ch=ch, pw=pw: e.matmul(px.ap[:, 0:pw], wa.ap[:, (1 * 2 + d_) * 8 + ch, :], xcb.ap[:, sl], start=True, stop=True),
                         reads=[wa, xcb], writes=[px])
                    K.op("act", lambda e, pa=pa, sl=sl, d_=d_, ch=ch, pw=pw: e.activation(out=ra.ap[:, sl], in_=pa.ap[:, 0:pw], func=AF.Sigmoid, bias=bab.ap[:, 0, d_, ch:ch + 1]),
                         reads=[pa, bab], writes=[ra])
                    K.op("act", lambda e, px=px, sl=sl, d_=d_, ch=ch, pw=pw: e.activation(out=ri.ap[:, sl], in_=px.ap[:, 0:pw], func=AF.Sigmoid, bias=bab.ap[:, 1, d_, ch:ch + 1]),
                         reads=[px, bab], writes=[ri])
                W = slice(0, Tn)
                K.op("act", lambda e, W=W, d_=d_, ch=ch: e.activation(out=ra.ap[:, W], in_=ra.ap[:, W], func=AF.Exp, scale=cl.ap[:, d_, ch:ch + 1]),
                     reads=[ra, cl], writes=[ra])
                K.op("pool", lambda e, W=W: e.tensor_tensor(out=ri.ap[:, W], in0=ri.ap[:, W], in1=xc.ap[:, W], op=ALU.mult), reads=[ri, xc], writes=[ri])
                K.op("dve", lambda e, W=W: e.tensor_tensor(out=sq.ap[:, W], in0=ra.ap[:, W], in1=ra.ap[:, W], op=ALU.mult), reads=[ra], writes=[sq])
                K.op("dve", lambda e, W=W: e.tensor_scalar(out=sq.ap[:, W], in0=sq.ap[:, W], scalar1=-1.0, scalar2=1.0, op0=ALU.mult, op1=ALU.add), reads=[sq], writes=[sq])
                K.op("act", lambda e, W=W: e.activation(out=sq.ap[:, W], in_=sq.ap[:, W], func=AF.Sqrt), reads=[sq], writes=[sq])
                K.op("pool", lambda e, W=W: e.tensor_tensor(out=ri.ap[:, W], in0=ri.ap[:, W], in1=sq.ap[:, W], op=ALU.mult), reads=[ri, sq], writes=[ri])
                init = h0.ap[:, d_, ch:ch + 1] if sid == 0 else zero.ap
                hsd = hs[d_]
                if d_ == 0:
                    K.op("dve", lambda e, W=W, hsd=hsd, init=init: e.tensor_tensor_scan(out=hsd.ap[:, W], data0=ra.ap[:, W], data1=ri.ap[:, W], initial=init,
                                                                                      op0=ALU.mult, op1=ALU.add), reads=[ra, ri, h0, zero], writes=[hsd])
                else:
                    rv = lambda t, Tn=Tn: bass.AP(t.ap.tensor, t.ap[:, Tn - 1:Tn].offset, [list(t.ap.ap[0]), [-1, Tn]])
                    K.op("dve", lambda e, hsd=hsd, init=init, rv=rv: e.tensor_tensor_scan(out=rv(hsd), data0=rv(ra), data1=rv(ri), initial=init,
                                                                                        op0=ALU.mult, op1=ALU.add), reads=[ra, ri, h0, zero], writes=[hsd])
                if sid > 0:
                    col = (Tn - 1) if d_ == 0 else 0
                    K.op("pool", lambda e, hsd=hsd, col=col, sid=sid, d_=d_, ch=ch: e.tensor_copy(out=fin.ap[:, sid - 1, d_, ch:ch + 1], in_=hsd.ap[:, col:col + 1]),
                         reads=[hsd], writes=[fin])
            W = slice(0, Tn)
            K.op("pool", lambda e, W=W: e.tensor_tensor(out=hs[0].ap[:, W], in0=hs[0].ap[:, W], in1=hs[1].ap[:, W], op=ALU.add), reads=[hs[0], hs[1]], writes=[hs[0]])
            K.op("dve", lambda e, W=W: e.tensor_tensor(out=ob.ap[:, W], in0=hs[0].ap[:, W], in1=gg.ap[:, W], op=ALU.mult), reads=[hs[0], gg], writes=[ob])
            K.dma("sp", S["OT"].ap[8 + ch, :, t0:t0 + Tn], ob.ap[:, W], reads=[ob])
    for s_ in range(2):
        for d_ in range(2):
            K.dma("sp", outs["nlru"][s_, d_].rearrange("(c p) -> p c", p=128), fin.ap[:, s_, d_, :], reads=[fin], allow_slow_non_contiguous=True)
    K.pop()


def stage_outproj(K, C, l, S_OT, woutD, XTin, XTout):
    j = 1
    K.push()
    w = K.tile([128, KC, KC, 128], BF16)
    K.dma("sp", w.ap.rearrange("p c k d -> p (c k d)"), woutD.ap, reads=[woutD], writes=[w])
    oT = [K.tile([128, KC, 512], BF16) for _ in range(2)]
    xres = [K.tile([128, 512], F32) for _ in range(2)]
    xo = [K.tile([128, 512], F32) for _ in range(2)]
    gate = C["gate"][l]
    for i in range(NTILE):
        cond = tile_cond(i)
        o_ = oT[i % 2]
        K.dma("sp", o_.ap, S_OT.ap[:, :, i * 512:(i + 1) * 512].rearrange("k p t -> p k t"), writes=[o_])
        for dc in range(KC):
            xr = xres[dc % 2]
            K.dma("sp", xr.ap, XTin.ap[dc * 128:(dc + 1) * 128, i * 512:(i + 1) * 512], writes=[xr])
            py = K.pbank[dc % 4]
            for k in range(KC):
                K.op("pe", lambda e, py=py, dc=dc, k=k, o_=o_: e.matmul(py.ap, w.ap[:, dc, k, :], o_.ap[:, k, :], start=(k == 0), stop=(k == KC - 1)),
                     reads=[w, o_], writes=[py])
            x_ = xo[dc % 2]
            K.op("dve", lambda e, x_=x_, py=py, xr=xr, dc=dc, cond=cond: e.scalar_tensor_tensor(
                out=x_.ap, in0=py.ap, scalar=gate.ap[:, j, dc, cond:cond + 1], in1=xr.ap, op0=ALU.mult, op1=ALU.add),
                reads=[py, xr, gate], writes=[x_])
            K.dma("sp", XTout.ap[dc * 128:(dc + 1) * 128, i * 512:(i + 1) * 512], x_.ap, reads=[x_])
    K.pop()


HC = 8
BIG = 30000.0


def mlstm_host_consts():
    nf = 64
    freqs = (10000.0 ** (-np.arange(nf, dtype=np.float32) / nf)).astype(np.float32)
    pos = np.arange(64, dtype=np.float32)
    ang = pos[None, :] * np.tile(freqs, 2)[:, None]
    cos = np.cos(ang).astype(np.float32)
    sin = np.sin(ang).astype(np.float32)
    sin[:64] *= -1.0
    t = np.arange(TS)
    rows, cols = t // 64, t % 64
    tabs = [cos[:, rows], sin[:, rows], cos[:, cols], sin[:, cols]]
    rope = np.stack(tabs + [x / 16.0 for x in tabs], axis=1).astype(np.float32)
    swap = np.zeros((128, 128), np.float32)
    for d in range(128):
        swap[d, (d + 64) % 128] = 1.0
    tri = np.zeros((2, 128, 128), np.float32)
    tri[0] = np.triu(np.ones((128, 128), np.float32))
    tri[1] = np.tril(np.ones((128, 128), np.float32))
    cmask = np.zeros((2, 128, 128), np.float32)
    cmask[0] = np.where(tri[0] > 0, 0.0, BIG)
    cmask[1] = np.where(tri[1] > 0, 0.0, BIG)
    return rope, swap, tri, cmask


def stage_inproj_c(K, C, ins, XT, S2, wD, l=1):
    j = 1
    K.push()
    fe = FrontEnd(K, C, K.pbank[4])
    hT = K.tile([128, KC, 512], BF16)
    wb = [K.tile([128, KC, 256], BF16) for _ in range(3)]
    wgt = K.tile([128, KC, 32], BF16)
    K.dma("pool", wgt.ap, ins["w_in_c"][:, 8192:8224].rearrange("(k p) f -> p k f", p=128), writes=[wgt])
    rope = K.tile([128, 8, 512], F32)
    swp = K.tile([128, 128], BF16)
    K.dma("pool", swp.ap, ins["swap_m"], writes=[swp])
    K.P.barrier()
    bg_col = K.tile([32, 1], F32)
    K.dma("sp", bg_col.ap, ins["b_gate_c"].rearrange("(p o) -> p o", o=1), writes=[bg_col])
    bg_bc = K.tile([128, 32], F32)
    K.dma("sp", bg_bc.ap, bcast_part(ins["b_gate_c"].rearrange("(o f) -> o f", o=1), 128), writes=[bg_bc])
    xb = [K.tile([128, 512], BF16) for _ in range(2)]
    t1 = [K.tile([128, 512], F32) for _ in range(2)]
    t2 = [K.tile([128, 512], F32) for _ in range(2)]
    ob = [K.tile([128, 512], BF16) for _ in range(2)]
    kst = [K.tile([128, 4, 128], BF16) for _ in range(2)]
    vb = [K.tile([128, 256], BF16) for _ in range(2)]
    gr = K.tile([32, 512], F32)
    gl = K.tile([32, 512], F32)
    gt = [K.tile([128, 32], F32) for _ in range(2)]
    gt2 = [K.tile([128, 32], F32) for _ in range(2)]
    cnt = 0
    import os
    dbg_tiles = [int(x) for x in os.environ.get("INPROJ_TILES", "0,1,2,3,4,5,6,7,8").split(",")]
    dbg_segs = [int(x) for x in os.environ.get("INPROJ_SEGS", "0,1,2,3,4,5").split(",")]
    for i in range(NTILE):
        if i not in dbg_tiles:
            continue
        cond = tile_cond(i)
        sample = i < TS // 512
        tok = slice(i * 512, (i + 1) * 512)
        fe.load(XT, i)
        fe.stats()
        fe.modulate(hT, l, j, cond)
        if sample:
            K.dma("sp", rope.ap, ins["rope_tab"][:, :, i * 512:(i + 1) * 512], writes=[rope])
        for n in range(32):
            if n // 8 not in dbg_segs:
                continue
            w = wb[n % 3]
            K.dma("sp", w.ap.rearrange("p k f -> p (k f)"), wD[n].ap, reads=[wD[n]], writes=[w])
            seg, h = n // 8, n % 8
            if seg == 2:
                for blk in range(4):
                    ps = K.pbank[5 + blk % 2]
                    for k in range(KC):
                        K.op("pe", lambda e, ps=ps, k=k, blk=blk, w=w: e.matmul(ps.ap[:, 0:256], hT.ap[:, k, blk * 128:(blk + 1) * 128], w.ap[:, k, :],
                                                                               start=(k == 0), stop=(k == KC - 1)), reads=[hT, w], writes=[ps])
                    v_ = vb[cnt % 2]
                    K.op("act", lambda e, v_=v_, ps=ps: e.copy(out=v_.ap, in_=ps.ap[:, 0:256]), reads=[ps], writes=[v_])
                    K.dma("sp", S2["Vtm"].ap[i * 512 + blk * 128:i * 512 + (blk + 1) * 128, h * 256:(h + 1) * 256], v_.ap, reads=[v_])
                    cnt += 1
                continue
            for dch in range(2):
                ps = K.pbank[cnt % 2]
                for k in range(KC):
                    K.op("pe", lambda e, ps=ps, k=k, dch=dch, w=w: e.matmul(ps.ap, w.ap[:, k, dch * 128:(dch + 1) * 128], hT.ap[:, k, :],
                                                                           start=(k == 0), stop=(k == KC - 1)), reads=[hT, w], writes=[ps])
                o_ = ob[cnt % 2]
                if seg == 3:
                    K.op("act", lambda e, o_=o_, ps=ps: e.activation(out=o_.ap, in_=ps.ap, func=AF.Sigmoid), reads=[ps], writes=[o_])
                    K.dma("sp", S2["SO"].ap[h * 2 + dch, :, tok], o_.ap, reads=[o_])
                    cnt += 1
                    continue
                if sample:
                    x_, a_, b_ = xb[cnt % 2], t1[cnt % 2], t2[cnt % 2]
                    ps2 = K.pbank[2 + cnt % 2]
                    tb = (0 if seg == 0 else 4) + 2 * dch
                    cosA, sinA = rope.ap[:, tb, :], rope.ap[:, tb + 1, :]
                    K.op("act", lambda e, a_=a_, ps=ps: e.activation(out=a_.ap, in_=ps.ap, func=AF.Identity), reads=[ps], writes=[a_])
                    K.op("pool", lambda e, x_=x_, a_=a_: e.tensor_copy(out=x_.ap, in_=a_.ap), reads=[a_], writes=[x_])
                    K.op("pe", lambda e, ps2=ps2, x_=x_: e.matmul(ps2.ap, swp.ap, x_.ap, start=True, stop=True), reads=[swp, x_], writes=[ps2])
                    K.op("act", lambda e, b_=b_, ps2=ps2: e.activation(out=b_.ap, in_=ps2.ap, func=AF.Identity), reads=[ps2], writes=[b_])
                    K.op("dve", lambda e, a_=a_, cosA=cosA: e.tensor_tensor(out=a_.ap, in0=a_.ap, in1=cosA, op=ALU.mult), reads=[a_, rope], writes=[a_])
                    K.op("dve", lambda e, b_=b_, sinA=sinA: e.tensor_tensor(out=b_.ap, in0=b_.ap, in1=sinA, op=ALU.mult), reads=[b_, rope], writes=[b_])
                    K.op("pool", lambda e, a_=a_, b_=b_, o_=o_: e.tensor_tensor(out=o_.ap, in0=a_.ap, in1=b_.ap, op=ALU.add), reads=[a_, b_], writes=[o_])
                else:
                    sc = 1.0 if seg == 0 else 1.0 / 16.0
                    K.op("act", lambda e, o_=o_, ps=ps, sc=sc: e.mul(out=o_.ap, in_=ps.ap, mul=sc), reads=[ps], writes=[o_])
                dst = S2["QT2"] if seg == 0 else S2["KT2"]
                K.dma("sp", dst.ap[h, dch, :, tok], o_.ap, reads=[o_])
                if seg == 1:
                    pt = K.pbank[7]
                    for blk in range(4):
                        K.op("pe", lambda e, pt=pt, o_=o_, blk=blk: e.matmul(pt.ap[:, blk * 128:(blk + 1) * 128], o_.ap[:, blk * 128:(blk + 1) * 128],
                                                                            C["ident_b"].ap, start=True, stop=True), reads=[o_, C["ident_b"]], writes=[pt])
                    s_ = kst[cnt % 2]
                    K.op("act", lambda e, s_=s_, pt=pt: e.copy(out=s_.ap.rearrange("p b d -> p (b d)"), in_=pt.ap), reads=[pt], writes=[s_])
                    K.dma("sp", S2["Ktm"].ap[i * 512:(i + 1) * 512, h, dch * 128:(dch + 1) * 128].rearrange("(b p) d -> p b d", p=128), s_.ap, reads=[s_])
                cnt += 1
        if 4 not in dbg_segs:
            continue
        pg = K.pbank[6]
        for k in range(KC):
            K.op("pe", lambda e, k=k: e.matmul(pg.ap[0:32, :], wgt.ap[:, k, :], hT.ap[:, k, :], start=(k == 0), stop=(k == KC - 1)),
                 reads=[wgt, hT], writes=[pg])
        K.op("act", lambda e: e.activation(out=gr.ap, in_=pg.ap[0:32, :], func=AF.Identity, bias=bg_col.ap), reads=[pg, bg_col], writes=[gr])
        K.dma("sp", S2["G_raw"].ap[:, tok], gr.ap, reads=[gr])
        K.op("act", lambda e: e.activation(out=gl.ap, in_=gr.ap, func=AF.Exp, scale=-1.0), reads=[gr], writes=[gl])
        K.op("dve", lambda e: e.tensor_scalar(out=gl.ap, in0=gl.ap, scalar1=1.0, scalar2=None, op0=ALU.add), reads=[gl], writes=[gl])
        K.op("act", lambda e: e.activation(out=gl.ap, in_=gl.ap, func=AF.Ln), reads=[gl], writes=[gl])
        K.op("dve", lambda e: e.tensor_scalar(out=gl.ap, in0=gl.ap, scalar1=-1.0, scalar2=None, op0=ALU.mult), reads=[gl], writes=[gl])
        K.dma("sp", S2["G_ls"].ap[:, tok], gl.ap, reads=[gl])
        if 5 not in dbg_segs:
            continue
        for blk in range(4):
            pc = K.pbank[5]
            for k in range(KC):
                K.op("pe", lambda e, k=k, blk=blk, pc=pc: e.matmul(pc.ap[:, 0:32], hT.ap[:, k, blk * 128:(blk + 1) * 128], wgt.ap[:, k, :],
                                                                  start=(k == 0), stop=(k == KC - 1)), reads=[wgt, hT], writes=[pc])
            g_, g2 = gt[blk % 2], gt2[blk % 2]
            K.op("dve", lambda e, g_=g_, pc=pc: e.tensor_tensor(out=g_.ap, in0=pc.ap[:, 0:32], in1=bg_bc.ap, op=ALU.add), reads=[pc, bg_bc], writes=[g_])
            K.op("act", lambda e, g_=g_, g2=g2: e.activation(out=g2.ap, in_=g_.ap, func=AF.Exp, scale=-1.0), reads=[g_], writes=[g2])
            K.op("dve", lambda e, g2=g2: e.tensor_scalar(out=g2.ap, in0=g2.ap, scalar1=1.0, scalar2=None, op0=ALU.add), reads=[g2], writes=[g2])
            K.op("act", lambda e, g2=g2: e.activation(out=g2.ap, in_=g2.ap, func=AF.Ln), reads=[g2], writes=[g2])
            for c0 in (8, 24):
                K.op("dve", lambda e, g_=g_, g2=g2, c0=c0: e.tensor_scalar(out=g_.ap[:, c0:c0 + 8], in0=g2.ap[:, c0:c0 + 8], scalar1=-1.0, scalar2=None, op0=ALU.mult),
                     reads=[g2, g_], writes=[g_])
            K.dma("sp", S2["G_tm"].ap[i * 512 + blk * 128:i * 512 + (blk + 1) * 128, :], g_.ap, reads=[g_])
    K.pop()


def ap_bc_last(ap, n):
    return bass.AP(ap.tensor, ap.offset, [list(x) for x in ap.ap] + [[0, n]])


def ap_bc_mid(ap, n):
    return bass.AP(ap.tensor, ap.offset, [list(ap.ap[0]), [0, n]] + [list(x) for x in ap.ap[1:]])


def ap_rev(ap):
    (ps, pn), (s, n) = ap.ap
    return bass.AP(ap.tensor, ap.offset + s * (n - 1), [[ps, pn], [-s, n]])


def stage_mlstm(K, C, ins, outs, S2):
    K.push()
    tri = K.tile([128, 2, 128], F32)
    cmask = K.tile([128, 2, 128], F32)
    K.dma("sp", tri.ap, ins["tri_m"].rearrange("d s t -> s d t"), writes=[tri])
    K.dma("sp", cmask.ap, ins["cmask_m"].rearrange("d s t -> s d t"), writes=[cmask])
    onesF = K.tile([128, 128], F32)
    K.op("dve", lambda e: e.memset(onesF.ap, 1.0), writes=[onesF])
    onec = K.tile([128, 1], BF16)
    K.op("dve", lambda e: e.memset(onec.ap, 1.0), writes=[onec])
    rmask = K.tile([128, 2, 8, 512], F32)
    K.op("dve", lambda e: e.memset(rmask.ap, 1.0), writes=[rmask])
    for d_ in range(2):
        col = 0 if d_ == 0 else 127
        K.op("dve", lambda e, d_=d_, col=col: e.memset(rmask.ap[:, d_, :, :].rearrange("p h (c t) -> p (h c) t", t=128)[:, :, col:col + 1], 0.0),
             reads=[rmask], writes=[rmask])
    NCH = NT // 128
    gtm = K.tile([128, NCH, 32], F32)
    K.dma("sp", gtm.ap, S2["G_tm"].ap.rearrange("(c p) g -> p c g", p=128), writes=[gtm])
    acol = K.tile([128, NCH, 16], F32)
    for c in range(NCH):
        pa = K.pbank[c % 2]
        for d_ in range(2):
            K.op("pe", lambda e, pa=pa, c=c, d_=d_: e.matmul(pa.ap[:, d_ * 8:d_ * 8 + 8], tri.ap[:, d_, :], gtm.ap[:, c, 16 * d_ + 8:16 * d_ + 16], start=True, stop=True),
                 reads=[tri, gtm], writes=[pa])
        K.op("dve", lambda e, pa=pa, c=c: e.tensor_tensor(out=acol.ap[:, c, :].rearrange("p (d h) -> p d h", d=2),
                                                          in0=gtm.ap[:, c, :].rearrange("p (d g h) -> p d g h", d=2, g=2)[:, :, 0, :],
                                                          in1=pa.ap[:, 0:16].rearrange("p (d h) -> p d h", d=2), op=ALU.subtract),
             reads=[pa, gtm], writes=[acol])
    lfB = K.tile([128, 8, 512], F32)
    aB = K.tile([128, 8, 512], F32)
    bcB = K.tile([128, 8, 512], F32)
    MB = K.tile([128, 8, 128], F32)
    gB = K.tile([128, 8, 128], F32)
    emB = K.tile([128, 8, 128], F32)
    MBm = K.tile([128, 8, 128], F32)
    WT = K.tile([128, 8, 128], F32)
    SW = K.tile([128, 8, 128], BF16)
    qg = K.tile([128, 8, 2, 128], BF16)
    qT = [K.tile([128, 16, 128], BF16) for _ in range(2)]
    kT = [K.tile([128, 16, 128], BF16) for _ in range(2)]
    ktm = [K.tile([128, 8, 256], BF16) for _ in range(2)]
    vtm = [K.tile([128, 8, 256], BF16) for _ in range(2)]
    kw = K.tile([128, 8, 256], BF16)
    dd = K.tile([128, 4, 128], F32)
    ho = [K.tile([128, 8, 128], F32) for _ in range(2)]
    Cst = K.tile([128, 8, 2, 256], F32)
    Cb = K.tile([128, 8, 2, 256], BF16)
    nst = K.tile([128, 8, 2], F32)
    nrep = K.tile([128, 8, 2, 128], BF16)
    mprev = K.tile([128, 8], F32)
    mnew = K.tile([128, 8], F32)
    we = K.tile([128, 8], F32)
    it = 0
    segs = [(0, TS, 0), (TS, 256, 1), (TS + 256, 256, 2)]
    for (t0, Tn, sid) in segs:
        nch = Tn // 128
        GW = min(512, Tn)
        for d_ in range(2):
            HD = S2["HF"] if d_ == 0 else S2["HB"]
            last = 127 if d_ == 0 else 0
            if sid == 0:
                for h in range(HC):
                    K.dma("sp", Cst.ap[:, h, :, :], ins["state_c"][d_, h].rearrange("(c p) e -> p c e", p=128), writes=[Cst])
                for h in range(HC):
                    K.dma("sp", nst.ap[:, h, :], ins["state_n"][d_, h].rearrange("(c p) -> p c", p=128), writes=[nst], allow_slow_non_contiguous=True)
                K.dma("sp", mprev.ap, bcast_part(ins["state_m"][d_:d_ + 1, :], 128), writes=[mprev])
            else:
                K.op("pool", lambda e: e.memset(Cst.ap, 0.0), writes=[Cst])
                K.op("pool", lambda e: e.memset(nst.ap, 0.0), writes=[nst])
                K.op("pool", lambda e: e.memset(mprev.ap, 0.0), writes=[mprev])
            K.op("act", lambda e: e.copy(out=Cb.ap.rearrange("p h c e -> p (h c e)"), in_=Cst.ap.rearrange("p h c e -> p (h c e)")), reads=[Cst], writes=[Cb])
            for h in range(HC):
                for dch in range(2):
                    K.op("dve", lambda e, h=h, dch=dch: e.tensor_scalar(out=nrep.ap[:, h, dch, :], in0=onesF.ap, scalar1=nst.ap[:, h, dch:dch + 1], scalar2=None, op0=ALU.mult),
                         reads=[nst, onesF], writes=[nrep])
            order = list(range(nch)) if d_ == 0 else list(range(nch - 1, -1, -1))
            prev_grp = -1
            for ci, c in enumerate(order):
                tc0 = t0 + c * 128
                cabs = tc0 // 128
                grp = (c * 128) // GW
                cg = (c * 128) % GW
                if grp != prev_grp:
                    prev_grp = grp
                    g0 = t0 + grp * GW
                    igrow, lfrow = (2 * d_) * 8, (2 * d_ + 1) * 8
                    src_ig = S2["G_raw"].ap[igrow:igrow + 8, g0:g0 + GW]
                    src_lf = S2["G_ls"].ap[lfrow:lfrow + 8, g0:g0 + GW]
                    bsrc = lambda a: bass.AP(a.tensor, a.offset, [[0, 128]] + [list(x) for x in a.ap])
                    K.dma("sp", aB.ap[:, :, 0:GW], bsrc(src_ig), writes=[aB])
                    K.dma("sp", lfB.ap[:, :, 0:GW], bsrc(src_lf), writes=[lfB])
                    for h in range(HC):
                        o_, m_, l_ = bcB.ap[:, h, 0:GW], rmask.ap[:, d_, h, 0:GW], lfB.ap[:, h, 0:GW]
                        if d_ == 1:
                            o_, m_, l_ = ap_rev(o_), ap_rev(m_), ap_rev(l_)
                        K.op("dve", lambda e, o_=o_, m_=m_, l_=l_: e.tensor_tensor_scan(out=o_, data0=m_, data1=l_, initial=0.0, op0=ALU.mult, op1=ALU.add),
                             reads=[rmask, lfB], writes=[bcB])
                    K.op("pool", lambda e, GW=GW: e.tensor_tensor(out=aB.ap[:, :, 0:GW], in0=aB.ap[:, :, 0:GW], in1=bcB.ap[:, :, 0:GW], op=ALU.subtract),
                         reads=[aB, bcB], writes=[aB])
                q_, k_, kt_, vt_ = qT[it % 2], kT[it % 2], ktm[it % 2], vtm[it % 2]
                K.dma("sp", q_.ap, S2["QT2"].ap[:, :, :, tc0:tc0 + 128].rearrange("h c p t -> p (h c) t"), writes=[q_])
                K.dma("sp", k_.ap, S2["KT2"].ap[:, :, :, tc0:tc0 + 128].rearrange("h c p t -> p (h c) t"), writes=[k_])
                K.dma("sp", kt_.ap, S2["Ktm"].ap[tc0:tc0 + 128, :, :], writes=[kt_])
                K.dma("sp", vt_.ap.rearrange("p h e -> p (h e)"), S2["Vtm"].ap[tc0:tc0 + 128, :], writes=[vt_])
                csl = slice(cg, cg + 128)
                for h in range(HC):
                    o_, a_ = MB.ap[:, h, :], aB.ap[:, h, csl]
                    if d_ == 1:
                        o_, a_ = ap_rev(o_), ap_rev(a_)
                    K.op("dve", lambda e, o_=o_, a_=a_, h=h: e.tensor_tensor_scan(out=o_, data0=onesF.ap, data1=a_, initial=mprev.ap[:, h:h + 1], op0=ALU.mult, op1=ALU.max),
                         reads=[aB, mprev, onesF], writes=[MB])
                K.op("dve", lambda e, cg=cg, last=last: e.tensor_tensor(out=mnew.ap, in0=bcB.ap[:, :, cg + last], in1=MB.ap[:, :, last], op=ALU.add),
                     reads=[bcB, MB], writes=[mnew])
                for h in range(HC):
                    K.op("act", lambda e, h=h: e.activation(out=gB.ap[:, h, :], in_=MB.ap[:, h, :], func=AF.Exp, scale=-1.0, bias=mprev.ap[:, h:h + 1]),
                         reads=[MB, mprev], writes=[gB])
                K.op("pool", lambda e, csl=csl: e.tensor_tensor(out=emB.ap, in0=bcB.ap[:, :, csl], in1=MB.ap, op=ALU.add), reads=[bcB, MB], writes=[emB])
                K.op("act", lambda e: e.activation(out=emB.ap, in_=emB.ap, func=AF.Exp, scale=-1.0), reads=[emB], writes=[emB])
                for h in range(HC):
                    K.op("pool", lambda e, d_=d_, h=h: e.tensor_tensor(out=MBm.ap[:, h, :], in0=MB.ap[:, h, :], in1=cmask.ap[:, d_, :], op=ALU.add), reads=[MB, cmask], writes=[MBm])
                for h in range(HC):
                    K.op("act", lambda e, h=h, cabs=cabs, d_=d_: e.activation(out=WT.ap[:, h, :], in_=MBm.ap[:, h, :], func=AF.Exp, scale=-1.0,
                                                                              bias=acol.ap[:, cabs, d_ * 8 + h:d_ * 8 + h + 1]), reads=[MBm, acol], writes=[WT])
                    for dch in range(2):
                        K.op("pool", lambda e, h=h, q_=q_, dch=dch: e.tensor_tensor(out=qg.ap[:, h, dch, :], in0=q_.ap[:, 2 * h + dch, :], in1=gB.ap[:, h, :], op=ALU.mult),
                             reads=[q_, gB], writes=[qg])
                K.op("dve", lambda e, cabs=cabs, d_=d_, last=last: e.tensor_tensor(out=we.ap, in0=acol.ap[:, cabs, d_ * 8:d_ * 8 + 8], in1=MB.ap[:, :, last], op=ALU.subtract),
                     reads=[acol, MB], writes=[we])
                K.op("act", lambda e: e.activation(out=we.ap, in_=we.ap, func=AF.Exp), reads=[we], writes=[we])
                for h in range(HC):
                    K.op("dve", lambda e, kt_=kt_, h=h: e.tensor_scalar(out=kw.ap[:, h, :], in0=kt_.ap[:, h, :], scalar1=we.ap[:, h:h + 1], scalar2=None, op0=ALU.mult),
                         reads=[kt_, we], writes=[kw])
                for half in range(2):
                    pS, pN0, pN1, pD = K.pbank[0], K.pbank[1], K.pbank[2], K.pbank[3]
                    pNv = T(K.psum[:, 512:1536].rearrange("p (a t) -> p a t", t=128))
                    for hh in range(4):
                        h = half * 4 + hh
                        for dch in range(2):
                            K.op("pe", lambda e, hh=hh, h=h, dch=dch, k_=k_, q_=q_, pS=pS: e.matmul(pS.ap[:, hh * 128:(hh + 1) * 128], k_.ap[:, 2 * h + dch, :], q_.ap[:, 2 * h + dch, :],
                                                                                                  start=(dch == 0), stop=(dch == 1)), reads=[k_, q_], writes=[pS])
                    hs4 = slice(half * 4, half * 4 + 4)
                    K.op("dve", lambda e, hs4=hs4, pS=pS: e.tensor_tensor(out=SW.ap[:, hs4, :], in0=pS.ap.rearrange("p (a t) -> p a t", t=128), in1=WT.ap[:, hs4, :], op=ALU.mult),
                         reads=[pS, WT], writes=[SW])
                    for hh in range(4):
                        h = half * 4 + hh
                        for jj in range(2):
                            pn_ = pN0 if hh < 2 else pN1
                            o_ = pNv.ap[:, hh * 2 + jj, :]
                            K.op("pe", lambda e, o_=o_, h=h, jj=jj, vt_=vt_: e.matmul(o_, vt_.ap[:, h, jj * 128:(jj + 1) * 128], SW.ap[:, h, :], start=True, stop=False),
                                 reads=[vt_, SW], writes=[pn_])
                            for dch in range(2):
                                K.op("pe", lambda e, o_=o_, h=h, jj=jj, dch=dch: e.matmul(o_, Cb.ap[:, h, dch, jj * 128:(jj + 1) * 128], qg.ap[:, h, dch, :], start=False, stop=(dch == 1)),
                                     reads=[Cb, qg], writes=[pn_])
                        od = pD.ap[:, hh * 128:(hh + 1) * 128]
                        K.op("pe", lambda e, od=od, h=h: e.matmul(od, C["ones_b"].ap, SW.ap[:, h, :], start=True, stop=False), reads=[C["ones_b"], SW], writes=[pD])
                        for dch in range(2):
                            K.op("pe", lambda e, od=od, h=h, dch=dch: e.matmul(od, nrep.ap[:, h, dch, :], qg.ap[:, h, dch, :], start=False, stop=(dch == 1)),
                                 reads=[nrep, qg], writes=[pD])
                    K.op("act", lambda e, pD=pD: e.activation(out=dd.ap.rearrange("p a t -> p (a t)"), in_=pD.ap, func=AF.Abs), reads=[pD], writes=[dd])
                    K.op("dve", lambda e, hs4=hs4: e.tensor_tensor(out=dd.ap, in0=dd.ap, in1=emB.ap[:, hs4, :], op=ALU.max), reads=[dd, emB], writes=[dd])
                    K.op("dve", lambda e: e.reciprocal(out=dd.ap, in_=dd.ap), reads=[dd], writes=[dd])
                    ho_ = ho[half]
                    for hh in range(4):
                        for jj in range(2):
                            K.op("dve", lambda e, hh=hh, jj=jj, ho_=ho_, pNv=pNv: e.tensor_tensor(out=ho_.ap[:, 2 * hh + jj, :], in0=pNv.ap[:, 2 * hh + jj, :],
                                                                                                 in1=dd.ap[:, hh, :], op=ALU.mult), reads=[pN0, pN1, dd], writes=[ho_])
                    K.dma("sp", HD.ap[half * 8:half * 8 + 8, :, tc0:tc0 + 128].rearrange("a p t -> p a t"), ho_.ap, reads=[ho_])
                for h in range(HC):
                    pC = K.pbank[4 + h % 2]
                    pn2 = K.pbank[6 + h % 2]
                    for dch in range(2):
                        K.op("pe", lambda e, pC=pC, h=h, dch=dch, vt_=vt_: e.matmul(pC.ap[:, dch * 256:(dch + 1) * 256], kw.ap[:, h, dch * 128:(dch + 1) * 128], vt_.ap[:, h, :],
                                                                                   start=True, stop=True), reads=[kw, vt_], writes=[pC])
                        K.op("pe", lambda e, pn2=pn2, h=h, dch=dch: e.matmul(pn2.ap[:, dch:dch + 1], kw.ap[:, h, dch * 128:(dch + 1) * 128], onec.ap, start=True, stop=True),
                             reads=[kw, onec], writes=[pn2])
                    gcol = gB.ap[:, h, last:last + 1]
                    K.op("dve", lambda e, pC=pC, h=h, gcol=gcol: e.scalar_tensor_tensor(out=Cst.ap[:, h, :, :].rearrange("p c e -> p (c e)"),
                                                                                       in0=Cst.ap[:, h, :, :].rearrange("p c e -> p (c e)"), scalar=gcol, in1=pC.ap,
                                                                                       op0=ALU.mult, op1=ALU.add), reads=[Cst, gB, pC], writes=[Cst])
                    K.op("act", lambda e, h=h: e.copy(out=Cb.ap[:, h, :, :], in_=Cst.ap[:, h, :, :]), reads=[Cst], writes=[Cb])
                    K.op("dve", lambda e, pn2=pn2, h=h, gcol=gcol: e.scalar_tensor_tensor(out=nst.ap[:, h, :], in0=nst.ap[:, h, :], scalar=gcol, in1=pn2.ap[:, 0:2],
                                                                                         op0=ALU.mult, op1=ALU.add), reads=[nst, gB, pn2], writes=[nst])
                    for dch in range(2):
                        K.op("pool", lambda e, h=h, dch=dch: e.tensor_scalar(out=nrep.ap[:, h, dch, :], in0=onesF.ap, scalar1=nst.ap[:, h, dch:dch + 1], scalar2=None, op0=ALU.mult),
                             reads=[nst, onesF], writes=[nrep])
                K.op("pool", lambda e: e.tensor_copy(out=mprev.ap, in_=mnew.ap), reads=[mnew], writes=[mprev])
                it += 1
            if sid > 0:
                for h in range(HC):
                    K.dma("sp", outs["nc"][sid - 1, d_, h].rearrange("(c p) e -> p c e", p=128), Cst.ap[:, h, :, :], reads=[Cst])
                    K.dma("sp", outs["nn"][sid - 1, d_, h].rearrange("(c p) -> p c", p=128), nst.ap[:, h, :], reads=[nst], allow_slow_non_contiguous=True)
                K.dma("sp", outs["nm"][sid - 1, d_:d_ + 1, :], mprev.ap[0:1, :], reads=[mprev])
    K.pop()


def stage_mlstm_post(K, C, ins, S2, OT):
    K.push()
    mhg = K.tile([128, 16], F32)
    K.dma("sp", mhg.ap, ins["mh_norm_g"].rearrange("(c p) -> p c", p=128), writes=[mhg], allow_slow_non_contiguous=True)
    f_ = K.tile([128, 16, 512], F32)
    b_ = K.tile([128, 16, 512], F32)
    s_ = K.tile([128, 16, 512], BF16)
    o_ = K.tile([128, 16, 512], BF16)
    sq = [K.tile([128, 512], BF16) for _ in range(2)]
    rt = [K.tile([128, 512], F32) for _ in range(2)]
    tm = [K.tile([128, 512], F32) for _ in range(2)]
    cnt = 0
    for i in range(NTILE):
        tok = slice(i * 512, (i + 1) * 512)
        K.dma("sp", f_.ap, S2["HF"].ap[:, :, tok].rearrange("a p t -> p a t"), writes=[f_])
        K.dma("sp", b_.ap, S2["HB"].ap[:, :, tok].rearrange("a p t -> p a t"), writes=[b_])
        K.dma("sp", s_.ap, S2["SO"].ap[:, :, tok].rearrange("a p t -> p a t"), writes=[s_])
        K.op("pool", lambda e: e.tensor_tensor(out=f_.ap, in0=f_.ap, in1=b_.ap, op=ALU.add), reads=[f_, b_], writes=[f_])
        for h in range(HC):
            ps = K.pbank[h % 2]
            for jj in range(2):
                q_ = sq[cnt % 2]
                K.op("act", lambda e, q_=q_, h=h, jj=jj: e.activation(out=q_.ap, in_=f_.ap[:, 2 * h + jj, :], func=AF.Square), reads=[f_], writes=[q_])
                K.op("pe", lambda e, ps=ps, q_=q_, jj=jj: e.matmul(ps.ap, C["ones_b"].ap, q_.ap, start=(jj == 0), stop=(jj == 1)), reads=[q_, C["ones_b"]], writes=[ps])
                cnt += 1
            r_ = rt[h % 2]
            K.op("dve", lambda e, r_=r_, ps=ps: e.tensor_scalar(out=r_.ap, in0=ps.ap, scalar1=1.0 / 256.0, scalar2=EPS, op0=ALU.mult, op1=ALU.add), reads=[ps], writes=[r_])
            K.op("act", lambda e, r_=r_: e.activation(out=r_.ap, in_=r_.ap, func=AF.Sqrt), reads=[r_], writes=[r_])
            K.op("dve", lambda e, r_=r_: e.reciprocal(out=r_.ap, in_=r_.ap), reads=[r_], writes=[r_])
            for jj in range(2):
                t_ = tm[jj]
                K.op("dve", lambda e, t_=t_, r_=r_, h=h, jj=jj: e.scalar_tensor_tensor(out=t_.ap, in0=f_.ap[:, 2 * h + jj, :], scalar=mhg.ap[:, 2 * h + jj:2 * h + jj + 1],
                                                                                       in1=r_.ap, op0=ALU.mult, op1=ALU.mult), reads=[f_, r_, mhg], writes=[t_])
                K.op("pool", lambda e, t_=t_, h=h, jj=jj: e.tensor_tensor(out=o_.ap[:, 2 * h + jj, :], in0=t_.ap, in1=s_.ap[:, 2 * h + jj, :], op=ALU.mult),
                     reads=[t_, s_], writes=[o_])
        K.dma("sp", OT.ap[:, :, tok].rearrange("a p t -> p a t"), o_.ap, reads=[o_])
    K.pop()


IN_SPECS = [
    ("xs", [TS, D]), ("xp", [TP, D]), ("cond", [2, D]),
    ("w_mod", [2, D, 9 * D]), ("b_mod", [2, 9 * D]), ("norm_g", [2, 3, D]),
    ("w_ffn_gate", [2, 2, D, DFF]), ("w_ffn_up", [2, 2, D, DFF]), ("w_ffn_down", [2, 2, DFF, D]),
    ("ident", [128, 128]),
    ("w_in_ab", [D, 5120]), ("w_out_ab", [D, D]), ("qn_g", [1, 128]), ("kn_g", [1, 128]),
    ("rpbT", [31, 120]), ("na_shift", [31, 64, 128]), ("na_mask", [5, 128, 640]),
    ("cache_k", [LCTX, 1024]), ("cache_v", [LCTX, 1024]), ("state_lru", [2, 1024]),
    ("conv_w", [4, 1024]), ("conv_b", [1024]), ("lru_wa", [2, 8, 128, 128]), ("lru_ba", [2, 1024]),
    ("lru_wx", [2, 8, 128, 128]), ("lru_bx", [2, 1024]), ("lru_lam", [2, 1024]),
    ("w_in_c", [D, 8224]), ("w_out_c", [D, D]), ("b_gate_c", [32]), ("mh_norm_g", [2048]),
    ("state_c", [2, 8, 256, 256]), ("state_n", [2, 8, 256]), ("state_m", [2, 8]),
    ("rope_tab", [128, 8, TS]), ("swap_m", [128, 128]), ("tri_m", [2, 128, 128]), ("cmask_m", [2, 128, 128]),
]
OUT_SPECS = [
    ("ys", [TS, D]), ("yp", [TP, D]), ("nk", [TP, 1024]), ("nv", [TP, 1024]), ("nlru", [2, 2, 1024]),
    ("nc", [2, 2, 8, 256, 256]), ("nn", [2, 2, 8, 256]), ("nm", [2, 2, 8]),
]
ALL_STAGES = ("ffn00", "inproj", "attn", "lru", "outproj", "ffn01", "mod1", "ffn10", "inprojc", "mlstm", "post", "outprojc", "ffn11")


def build(stages=ALL_STAGES):
    K = KB()
    nc = K.nc
    class LazyIns(dict):
        def __missing__(self, name):
            shape = dict(IN_SPECS)[name]
            v = nc.dram_tensor(name, shape, F32, kind="ExternalInput").ap()
            self[name] = v
            return v
    ins = LazyIns()
    outs = {name: nc.dram_tensor(name, shape, F32, kind="ExternalOutput").ap() for name, shape in OUT_SPECS}
    XT = [XTB(K.dram("XT%d" % i, [D, NT], F32)) for i in range(2)]
    S = {
        "QT": T(K.dram("QT", [HA, 128, NT], BF16)), "KT": T(K.dram("KT", [HA, 128, NT], BF16)),
        "V": T(K.dram("V", [NT, 1024], BF16)),
        "XB": T(K.dram("XB", [8, 128, NT], F32)), "GG": T(K.dram("GG", [8, 128, NT], F32)),
        "OT": T(K.dram("OT", [KC, 128, NT], BF16)),
    }
    S2 = {
        "QT2": T(K.dram("QT2", [HC, 2, 128, NT], BF16)), "KT2": T(K.dram("KT2", [HC, 2, 128, NT], BF16)),
        "Ktm": T(K.dram("Ktm", [NT, HC, 256], BF16)), "Vtm": T(K.dram("Vtm", [NT, 2048], BF16)),
        "SO": T(K.dram("SO", [KC, 128, NT], BF16)),
        "G_raw": T(K.dram("G_raw", [32, NT], F32)), "G_ls": T(K.dram("G_ls", [32, NT], F32)), "G_tm": T(K.dram("G_tm", [NT, 32], F32)),
        "HF": T(K.dram("HF", [KC, 128, NT], F32)), "HB": T(K.dram("HB", [KC, 128, NT], F32)),
    }
    C = stage_consts(K, ins)
    srcs = [ins["xs"][b * 128:(b + 1) * 128, :] for b in range(TS // 128)] + [ins["xp"][b * 128:(b + 1) * 128, :] for b in range(TP // 128)]
    dsts = [outs["ys"][b * 128:(b + 1) * 128, :] for b in range(TS // 128)] + [outs["yp"][b * 128:(b + 1) * 128, :] for b in range(TP // 128)]
    st = set(stages)
    w00 = precast_ffn(K, ins, 0, 0) if "ffn00" in st else None
    w_ab = precast_cols(K, ins["w_in_ab"], "w_in_ab_bf", 5120) if "inproj" in st else None
    wo_ab = precast_wout(K, ins["w_out_ab"], "w_out_ab_bf") if "outproj" in st else None
    w01 = precast_ffn(K, ins, 0, 1) if "ffn01" in st else None
    w10 = precast_ffn(K, ins, 1, 0) if "ffn10" in st else None
    w_c = precast_cols(K, ins["w_in_c"][:, 0:8192], "w_in_c_bf", 8192) if "inprojc" in st else None
    wo_c = precast_wout(K, ins["w_out_c"], "w_out_c_bf") if "outprojc" in st else None
    w11 = precast_ffn(K, ins, 1, 1) if "ffn11" in st else None
    cur = 0
    K.P.barrier()
    stage_to_fm(K, C, srcs, XT[cur])
    if "nomod" not in st:
        stage_mod(K, C, ins, 0)
    if "ffn00" in st:
        stage_ffn(K, C, 0, 0, XT[cur], XT[1 - cur], *w00)
        cur = 1 - cur
    if "inproj" in st:
        stage_inproj_ab(K, C, ins, outs, XT[cur], S, w_ab)
    if "attn" in st:
        stage_attention(K, C, ins, S)
    if "lru" in st:
        stage_lru(K, C, ins, outs, S)
    if "outproj" in st:
        stage_outproj(K, C, 0, S["OT"], wo_ab, XT[cur], XT[1 - cur])
        cur = 1 - cur
    if "ffn01" in st:
        stage_ffn(K, C, 0, 1, XT[cur], XT[1 - cur], *w01)
        cur = 1 - cur
    if "mod1" in st:
        stage_mod(K, C, ins, 1)
    if "ffn10" in st:
        stage_ffn(K, C, 1, 0, XT[cur], XT[1 - cur], *w10)
        cur = 1 - cur
    if "inprojc" in st:
        stage_inproj_c(K, C, ins, XT[cur], S2, w_c)
    if "mlstm" in st:
        stage_mlstm(K, C, ins, outs, S2)
    if "post" in st:
        stage_mlstm_post(K, C, ins, S2, S["OT"])
    if "outprojc" in st:
        stage_outproj(K, C, 1, S["OT"], wo_c, XT[cur], XT[1 - cur])
        cur = 1 - cur
    if "ffn11" in st:
        stage_ffn(K, C, 1, 1, XT[cur], XT[1 - cur], *w11)
        cur = 1 - cur
    stage_to_tm(K, C, XT[cur], dsts)
    K.P.emit()
    nc._used_inputs = list(ins.keys())
    return nc


def make_in_maps(inputs, n_cores=8):
    ident = np.eye(128, dtype=np.float32)
    shift, mask = na_host_consts()
    rope, swap, tri, cmask = mlstm_host_consts()
    rpbT = np.ascontiguousarray(np.transpose(inputs["rpb"][0][:, ::-1, :], (2, 0, 1)).reshape(31, 120))
    shared = {k: np.ascontiguousarray(inputs[k]) for k in ("w_mod", "b_mod", "norm_g", "w_ffn_gate", "w_ffn_up", "w_ffn_down", "qn_g", "kn_g")}
    shared.update({
        "ident": ident, "na_shift": shift, "na_mask": mask, "rpbT": rpbT,
        "rope_tab": rope, "swap_m": swap, "tri_m": tri, "cmask_m": cmask,
        "w_in_ab": np.ascontiguousarray(inputs["w_in_ab"][0]), "w_out_ab": np.ascontiguousarray(inputs["w_out_ab"][0]),
        "conv_w": np.ascontiguousarray(inputs["conv_w"][0]), "conv_b": np.ascontiguousarray(inputs["conv_b"][0]),
        "lru_wa": np.ascontiguousarray(inputs["lru_wa"][0]), "lru_ba": np.ascontiguousarray(inputs["lru_ba"][0]),
        "lru_wx": np.ascontiguousarray(inputs["lru_wx"][0]), "lru_bx": np.ascontiguousarray(inputs["lru_bx"][0]),
        "lru_lam": np.ascontiguousarray(inputs["lru_lam"][0]),
        "w_in_c": np.ascontiguousarray(inputs["w_in_c"][0]), "w_out_c": np.ascontiguousarray(inputs["w_out_c"][0]),
        "b_gate_c": np.ascontiguousarray(inputs["b_gate_c"][0].reshape(32)),
        "mh_norm_g": np.ascontiguousarray(inputs["mh_norm_g"][0].reshape(2048)),
    })
    maps = []
    for b in range(n_cores):
        m = dict(shared)
        m.update({
            "xs": np.ascontiguousarray(inputs["x_sample"][b]),
            "xp": np.ascontiguousarray(inputs["x_prompt"][2 * b:2 * b + 2].reshape(TP, D)),
            "cond": np.ascontiguousarray(np.stack([inputs["c"][b], inputs["c_ctx"]], axis=0)),
            "cache_k": np.ascontiguousarray(inputs["cache_k"][b, 0].reshape(LCTX, 1024)),
            "cache_v": np.ascontiguousarray(inputs["cache_v"][b, 0].reshape(LCTX, 1024)),
            "state_lru": np.ascontiguousarray(inputs["state_lru"][b, 0]),
            "state_c": np.ascontiguousarray(inputs["state_mlstm_c"][b, 0]),
            "state_n": np.ascontiguousarray(inputs["state_mlstm_n"][b, 0]),
            "state_m": np.ascontiguousarray(inputs["state_mlstm_m"][b, 0]),
        })
        maps.append(m)
    return maps


def kernel(**inputs):
    inputs = {k: np.asarray(v, dtype=np.float32) for k, v in inputs.items()}
    nc = build()
    maps = make_in_maps(inputs)
    maps = [{k: v for k, v in m.items() if k in nc._used_inputs} for m in maps]
    res = run_bass_kernel_spmd(nc, maps, core_ids=list(range(8)))
    r = res.results
    f32 = np.float32
    y_prompt = np.concatenate([r[b]["yp"].reshape(2, 256, D) for b in range(8)], axis=0).astype(f32)
    y_sample = np.stack([r[b]["ys"] for b in range(8)], axis=0).astype(f32)
    new_k = np.concatenate([r[b]["nk"].reshape(2, 1, 256, 8, 128) for b in range(8)], axis=0).astype(f32)
    new_v = np.concatenate([r[b]["nv"].reshape(2, 1, 256, 8, 128) for b in range(8)], axis=0).astype(f32)
    new_lru = np.concatenate([r[b]["nlru"].reshape(2, 1, 2, 1024) for b in range(8)], axis=0).astype(f32)
    new_c = np.concatenate([r[b]["nc"].reshape(2, 1, 2, 8, 256, 256) for b in range(8)], axis=0).astype(f32)
    new_n = np.concatenate([r[b]["nn"].reshape(2, 1, 2, 8, 256) for b in range(8)], axis=0).astype(f32)
    new_m = np.concatenate([r[b]["nm"].reshape(2, 1, 2, 8) for b in range(8)], axis=0).astype(f32)
    return (y_prompt, y_sample, new_k, new_v, new_lru, new_c, new_n, new_m)
```
